# Optimizing a Trainium2 kernel written in Bass

```python
import jax, jax.numpy as jnp
from jax import lax
import numpy as np

D_MODEL = 2048
BATCH = 8
SEQ = 4096
DEPTH = 4

N_MEM = 256
N_MIXERS = 3
N_A_LAYERS = (DEPTH + 2) // 3
N_B_LAYERS = (DEPTH + 1) // 3
N_C_LAYERS = DEPTH // 3
SHORT_CONV = 3
CHUNK = 128
GMLP_GROUPS = 8
GMLP_HIDDEN = D_MODEL
GMLP_GROUP_DIM = GMLP_HIDDEN // GMLP_GROUPS
CONF_CONV = 31
XA_HEADS = 4
XA_HEAD_DIM = D_MODEL // XA_HEADS
D_FF = ((8 * D_MODEL + 3 * 256 - 1) // (3 * 256)) * 256
EPS = 1e-6

kernel_name = 'hybrid_interleaved_conv_gmlp_conformer_xattn'


def _rmsnorm(x, g):
    xf = x.astype(jnp.float32)
    y = xf * lax.rsqrt(jnp.mean(xf * xf, axis=-1, keepdims=True) + EPS)
    return (y * g.astype(jnp.float32)).astype(x.dtype)


def _layernorm(x, g, b):
    xf = x.astype(jnp.float32)
    mu = jnp.mean(xf, axis=-1, keepdims=True)
    var = jnp.mean(jnp.square(xf - mu), axis=-1, keepdims=True)
    y = (xf - mu) * lax.rsqrt(var + EPS)
    return (y * g.astype(jnp.float32) + b.astype(jnp.float32)).astype(x.dtype)


def _causal_dwconv(x, w):
    k = w.shape[0]
    return lax.conv_general_dilated(
        x, w[:, None, :].astype(x.dtype), window_strides=(1,),
        padding=[(k - 1, 0)], dimension_numbers=('NWC', 'WIO', 'NWC'),
        feature_group_count=x.shape[-1])


def _mixer_short_conv(h, w_in, conv_w, w_out):
    bcz = h @ w_in
    b_gate, c_gate, z = jnp.split(bcz, 3, axis=-1)
    y = _causal_dwconv(c_gate * z, conv_w)
    return (b_gate * y) @ w_out


def _mixer_chunked_gmlp(h, w_in, v_g, v_b, w_s, s_bias, w_out):
    bsz, seq, _ = h.shape
    uv = jax.nn.gelu(h @ w_in)
    u, v = jnp.split(uv, 2, axis=-1)
    v = _layernorm(v, v_g, v_b)
    v = v.reshape(bsz, seq // CHUNK, CHUNK, GMLP_GROUPS, GMLP_GROUP_DIM)
    mask = jnp.tril(jnp.ones((CHUNK, CHUNK), dtype=bool))
    ws = jnp.where(mask[None], w_s, jnp.zeros((), w_s.dtype))
    sv = jnp.einsum('gts,bnsgc->bntgc', ws, v)
    sv = sv + s_bias.T[None, None, :, :, None]
    gated = u * sv.reshape(bsz, seq, GMLP_HIDDEN)
    return gated @ w_out


def _mixer_conformer_conv(h, w_in, conv_w, conv_b, ln_g, ln_b, w_out):
    ag = h @ w_in
    a, g = jnp.split(ag, 2, axis=-1)
    y = a * jax.nn.sigmoid(g)
    y = _causal_dwconv(y, conv_w) + conv_b
    y = _layernorm(y, ln_g, ln_b)
    y = jax.nn.silu(y)
    return y @ w_out


def _cross_attention(h, mem_n, wq, wkv, wo):
    bsz, seq, _ = h.shape
    q = (h @ wq).reshape(bsz, seq, XA_HEADS, XA_HEAD_DIM)
    kv = mem_n @ wkv
    k, v = jnp.split(kv, 2, axis=-1)
    k = k.reshape(bsz, N_MEM, XA_HEADS, XA_HEAD_DIM)
    v = v.reshape(bsz, N_MEM, XA_HEADS, XA_HEAD_DIM)
    scale = XA_HEAD_DIM ** -0.5
    s = jnp.einsum('bshd,bmhd->bhsm', q, k).astype(jnp.float32) * scale
    p = jax.nn.softmax(s, axis=-1).astype(v.dtype)
    o = jnp.einsum('bhsm,bmhd->bshd', p, v).reshape(bsz, seq, D_MODEL)
    return o @ wo


def _swiglu(h, w_gu, w_down):
    gu = h @ w_gu
    gate, up = jnp.split(gu, 2, axis=-1)
    return (jax.nn.silu(gate) * up) @ w_down


def setup_inputs(seed: int = 0) -> dict:
    key = jax.random.key(seed)
    ks = jax.random.split(key, 32)

    def nrm(k, shape, scale):
        return jax.random.normal(k, shape, jnp.float32) * scale

    def gain(k, shape):
        return 1.0 + 0.05 * jax.random.normal(k, shape, jnp.float32)

    d = D_MODEL
    return {
        'x': nrm(ks[0], (BATCH, SEQ, d), 1.0),
        'mem': nrm(ks[1], (BATCH, N_MEM, d), 1.0),
        'mix_norm': gain(ks[2], (DEPTH, 2, d)),
        'xa_norm': gain(ks[3], (DEPTH, 3, d)),
        'xa_wq': nrm(ks[4], (DEPTH, d, d), d ** -0.5),
        'xa_wkv': nrm(ks[5], (DEPTH, d, 2 * d), d ** -0.5),
        'xa_wo': nrm(ks[6], (DEPTH, d, d), d ** -0.5),
        'ffn_norm': gain(ks[7], (DEPTH, 2, d)),
        'ffn_w_gu': nrm(ks[8], (DEPTH, d, 2 * D_FF), d ** -0.5),
        'ffn_w_down': nrm(ks[9], (DEPTH, D_FF, d), D_FF ** -0.5),
        'a_w_in': nrm(ks[10], (N_A_LAYERS, d, 3 * d), d ** -0.5),
        'a_conv_w': nrm(ks[11], (N_A_LAYERS, SHORT_CONV, d), SHORT_CONV ** -0.5),
        'a_w_out': nrm(ks[12], (N_A_LAYERS, d, d), d ** -0.5),
        'b_w_in': nrm(ks[13], (N_B_LAYERS, d, 2 * GMLP_HIDDEN), d ** -0.5),
        'b_v_g': gain(ks[14], (N_B_LAYERS, GMLP_HIDDEN)),
        'b_v_b': nrm(ks[15], (N_B_LAYERS, GMLP_HIDDEN), 0.02),
        'b_w_s': nrm(ks[16], (N_B_LAYERS, GMLP_GROUPS, CHUNK, CHUNK), CHUNK ** -0.5),
        'b_s_bias': gain(ks[17], (N_B_LAYERS, GMLP_GROUPS, CHUNK)),
        'b_w_out': nrm(ks[18], (N_B_LAYERS, GMLP_HIDDEN, d), GMLP_HIDDEN ** -0.5),
        'c_w_in': nrm(ks[19], (N_C_LAYERS, d, 2 * d), d ** -0.5),
        'c_conv_w': nrm(ks[20], (N_C_LAYERS, CONF_CONV, d), CONF_CONV ** -0.5),
        'c_conv_b': nrm(ks[21], (N_C_LAYERS, d), 0.02),
        'c_ln_g': gain(ks[22], (N_C_LAYERS, d)),
        'c_ln_b': nrm(ks[23], (N_C_LAYERS, d), 0.02),
        'c_w_out': nrm(ks[24], (N_C_LAYERS, d, d), d ** -0.5),
    }


def reference(x, mem, mix_norm, xa_norm, xa_wq, xa_wkv, xa_wo, ffn_norm,
              ffn_w_gu, ffn_w_down, a_w_in, a_conv_w, a_w_out,
              b_w_in, b_v_g, b_v_b, b_w_s, b_s_bias, b_w_out,
              c_w_in, c_conv_w, c_conv_b, c_ln_g, c_ln_b, c_w_out):
    for i in range(DEPTH):
        kind = i % N_MIXERS
        slot = i // N_MIXERS
        h = _rmsnorm(x, mix_norm[i, 0])
        if kind == 0:
            y = _mixer_short_conv(h, a_w_in[slot], a_conv_w[slot], a_w_out[slot])
        elif kind == 1:
            y = _mixer_chunked_gmlp(h, b_w_in[slot], b_v_g[slot], b_v_b[slot],
                                    b_w_s[slot], b_s_bias[slot], b_w_out[slot])
        else:
            y = _mixer_conformer_conv(h, c_w_in[slot], c_conv_w[slot], c_conv_b[slot],
                                      c_ln_g[slot], c_ln_b[slot], c_w_out[slot])
        x = x + _rmsnorm(y, mix_norm[i, 1])
        h = _rmsnorm(x, xa_norm[i, 0])
        mem_n = _rmsnorm(mem, xa_norm[i, 2])
        y = _cross_attention(h, mem_n, xa_wq[i], xa_wkv[i], xa_wo[i])
        x = x + _rmsnorm(y, xa_norm[i, 1])
        h = _rmsnorm(x, ffn_norm[i, 0])
        y = _swiglu(h, ffn_w_gu[i], ffn_w_down[i])
        x = x + _rmsnorm(y, ffn_norm[i, 1])
    return x
```

```python
import numpy as np
from contextlib import ExitStack
import concourse.bass as bass
import concourse.mybir as mybir
from concourse.bass_utils import run_bass_kernel_spmd

F32 = mybir.dt.float32
BF16 = mybir.dt.bfloat16
AF = mybir.ActivationFunctionType
ALU = mybir.AluOpType

D = 2048
DC = 16
T = 512
SEQ = 4096
NMEM = 256
DFF = 5632
FC = 44
EPS = 1e-6
NV = 70
NSLOT = 4
NWARM = 40
WCOLS = 4096
SAME_SYNC = True
RSD = float(D) ** -0.5


class Ctx:
    def __init__(self, nc, stack):
        self.nc = nc
        self.eng = {'pe': nc.tensor, 'act': nc.scalar, 'dve': nc.vector,
                    'pool': nc.gpsimd, 'sp': nc.sync}
        self.sems = {}
        self.ecnt = {e: 0 for e in self.eng}
        self.dcnt = {}
        self.waited = {e: {} for e in self.eng}
        self.last_w = {}
        self.readers = {}
        self.stack = stack
        for e in self.eng:
            self.sems['e_' + e] = stack.enter_context(nc.semaphore('s_e_' + e))
        self.n_inst = 0
        self.n_wait = 0

    def dma_sem(self, name):
        self.sems[name] = self.stack.enter_context(self.nc.semaphore('s_' + name))
        self.dcnt[name] = 0

    def _waits(self, e, reads, writes):
        need = {}
        for r in reads:
            ev = self.last_w.get(r)
            if ev is not None and need.get(ev[0], 0) < ev[1]:
                need[ev[0]] = ev[1]
        for w in writes:
            ev = self.last_w.get(w)
            if ev is not None and need.get(ev[0], 0) < ev[1]:
                need[ev[0]] = ev[1]
            for k, v in self.readers.get(w, {}).items():
                if need.get(k, 0) < v:
                    need[k] = v
        own = 'e_' + e
        for k, v in need.items():
            if k == own and (e == 'pe' or e == 'sp' or not SAME_SYNC):
                continue
            if self.waited[e].get(k, 0) >= v:
                continue
            self.eng[e].wait_ge(self.sems[k], v)
            self.waited[e][k] = v
            self.n_wait += 1

    def _record(self, ev, reads, writes):
        k, v = ev
        for r in reads:
            d = self.readers.setdefault(r, {})
            if d.get(k, 0) < v:
                d[k] = v
        for w in writes:
            self.last_w[w] = ev
            self.readers[w] = {}

    def emit(self, e, fn, reads=(), writes=()):
        self._waits(e, reads, writes)
        inst = fn(self.eng[e])
        self.ecnt[e] += 1
        inst.then_inc(self.sems['e_' + e], 1)
        self._record(('e_' + e, self.ecnt[e]), reads, writes)
        self.n_inst += 1
        return inst

    def dma(self, q, semname, fn, reads=(), writes=()):
        self._waits(q, reads, writes)
        inst = fn(self.eng[q])
        self.dcnt[semname] += 16
        inst.then_inc(self.sems[semname], 16)
        self._record((semname, self.dcnt[semname]), reads, writes)
        self.n_inst += 1
        return inst

    def wait_all(self, e, resources):
        self._waits(e, list(resources), [])


def V_MIX(i, j): return i * 2 + j
def V_XA(i, j): return 8 + i * 3 + j
def V_FFN(i, j): return 20 + i * 2 + j
def V_ACONV(s, k): return 28 + s * 3 + k
def V_CCONV(k): return 34 + k
V_CB, V_CLG, V_CLB, V_BVG, V_BVB = 65, 66, 67, 68, 69


def build(NT, layers, stop=None, dbg=False):
    nc = bass.Bass("TRN2", target_bir_lowering=False)

    def din(name, shape, dt=F32):
        return nc.dram_tensor(name, shape, dt, kind="ExternalInput").ap()

    xT = din("xT", [NT, 128, DC * T])
    memT = din("memT", [128, DC * NMEM])
    vecs_d = din("vecs", [128, NV * DC])
    wsT_d = din("wsT", [128, 8 * 128])
    sbias_d = din("sbias", [128, 8 * 128])
    w_a_in = din("a_w_in", [2, 12, 2, 128, 4096])
    w_a_out = din("a_w_out", [2, 4, 2, 128, 4096])
    w_b_in = din("b_w_in", [1, 8, 2, 128, 4096])
    w_b_out = din("b_w_out", [1, 4, 2, 128, 4096])
    w_c_in = din("c_w_in", [1, 8, 2, 128, 4096])
    w_c_out = din("c_w_out", [1, 4, 2, 128, 4096])
    w_q = din("xa_wq", [4, 4, 2, 128, 4096])
    w_kv = din("xa_wkv", [4, 8, 2, 128, 4096])
    w_o = din("xa_wo", [4, 4, 2, 128, 4096])
    w_gu = din("ffn_w_gu", [4, 22, 2, 128, 4096])
    w_dn = din("ffn_w_down", [4, 8, 4, 128, 11 * 256])
    outT = nc.dram_tensor("outT", [NT, 128, DC * T], F32, kind="ExternalOutput").ap()
    kvc = nc.dram_tensor("kvc", [4, 128, 8192], BF16).ap()
    dgc = nc.dram_tensor("dgc", [DC, 128, 3968], BF16).ap()

    with ExitStack() as st:
        cx = Ctx(nc, st)
        for n in ['cst', 'kvst', 'kvld', 'mld', 'ost', 'dgst', 'dg0', 'dg1', 'dg2'] + ['w%d' % s for s in range(NSLOT)] + ['xld%d' % c for c in range(DC)] + ['ost%d' % c for c in range(DC)]:
            cx.dma_sem(n)

        def sb(name, cols, dt):
            return st.enter_context(nc.sbuf_tensor("sb_" + name, [128, cols], dt))

        xs = sb("xs", DC * T, F32)
        h = sb("h", DC * T, BF16)
        big = sb("big", FC * T, BF16)
        y = sb("y", DC * T, F32)
        gt = sb("gt", 4 * T, F32)
        wbuf = [sb("w%d" % s, WCOLS, BF16) for s in range(NSLOT)]
        rstd = sb("rstd", T, F32)
        rden = sb("rden", T, F32)
        rin = sb("rin", T, F32)
        meanb = rin
        onesf = sb("onesf", 1, F32)
        rcol = sb("rcol", 4, F32)
        tmp = [sb("tmp%d" % i, T, F32) for i in range(3)]
        sq = [sb("sq%d" % i, T, BF16) for i in range(4)]
        cbuf = [sb("cbuf%d" % i, 544, F32) for i in range(2)]
        pTT = sb("pTT", 2176, BF16)
        identb = sb("identb", 128, BF16)
        cbR = sb("cbR", DC, F32)
        vecs = sb("vecs", NV * DC, F32)
        ones = sb("ones", 128, BF16)
        wsT = sb("wsTb", 1024, BF16)
        Rb = sb("Rb", 1024, F32)
        sbias = sb("sbias", 1024, F32)
        haloA = sb("haloA", 2 * DC * 2, F32)
        haloC = sb("haloC", DC * 30, BF16)
        bst = sb("bst", 4 * 4 * 6, F32)
        mv = sb("mv", 8, F32)
        rs = sb("rs", 4, F32)
        ps = [st.enter_context(nc.psum_tensor("ps%d" % i, [128, 512], F32)) for i in range(8)]
        bank_ctr = [0]

        held = set()

        def bank():
            while True:
                b = bank_ctr[0] % 8
                bank_ctr[0] += 1
                if b not in held:
                    return b

        def vcol(vid, c):
            return vecs[:, vid * DC + c: vid * DC + c + 1]

        wq_list = []
        wstate = {'load': 0, 'use': 0}

        def w_prefetch():
            while wstate['load'] < len(wq_list) and wstate['load'] < wstate['use'] + NSLOT:
                i = wstate['load']
                ap, ncols = wq_list[i]
                s = i % NSLOT
                cx.dma('pool', 'w%d' % s,
                       lambda e, ap=ap, s=s, ncols=ncols: e.dma_start(out=wbuf[s][:, :ncols], in_=ap),
                       writes=[('w', s)])
                wstate['load'] += 1

        def w_get(ap):
            i = wstate['use']
            assert wq_list[i][0] is ap, "weight stream order mismatch at %d" % i
            w_prefetch()
            return i % NSLOT

        def w_done():
            wstate['use'] += 1
            w_prefetch()

        def plan_weights():
            def mixer_seq(i):
                kind, slot = i % 3, i // 3
                out = []
                if kind == 0:
                    for g in range(4):
                        out += [w_a_in[slot, g], w_a_in[slot, 4 + g], w_a_in[slot, 8 + g]]
                    out += [w_a_out[slot, n] for n in range(4)]
                elif kind == 1:
                    out += [w_b_in[slot, n] for n in range(8)]
                    out += [w_b_out[slot, n] for n in range(4)]
                else:
                    for g in range(4):
                        out += [w_c_in[slot, 4 + g], w_c_in[slot, g]]
                    out += [w_c_out[slot, n] for n in range(4)]
                return out
            seq = []
            for i in layers:
                for n in range(8):
                    seq += [(a, 4096) for a in w_kv[i, n]]
            per_tile = []
            for i in layers:
                for tl in mixer_seq(i):
                    per_tile += [(a, 4096) for a in tl]
                for n in range(4):
                    per_tile += [(a, 4096) for a in w_q[i, n]]
                for n in range(4):
                    per_tile += [(a, 4096) for a in w_o[i, n]]
                for n in range(22):
                    per_tile += [(a, 4096) for a in w_gu[i, n]]
                for n in range(8):
                    per_tile += [(a, 2816) for a in w_dn[i, n]]
            return seq, per_tile

        ap_cache = {}
        for (nm, t, shape) in [('a_in', w_a_in, (2, 12)), ('a_out', w_a_out, (2, 4)), ('b_in', w_b_in, (1, 8)),
                               ('b_out', w_b_out, (1, 4)), ('c_in', w_c_in, (1, 8)), ('c_out', w_c_out, (1, 4)),
                               ('q', w_q, (4, 4)), ('kv', w_kv, (4, 8)), ('o', w_o, (4, 4)), ('gu', w_gu, (4, 22))]:
            for a in range(shape[0]):
                for b in range(shape[1]):
                    ap_cache[(nm, a, b)] = [t[a, b, 0], t[a, b, 1]]
        for a in range(4):
            for b in range(8):
                ap_cache[('dn', a, b)] = [w_dn[a, b, c] for c in range(4)]

        class _W:
            def __init__(self, nm):
                self.nm = nm

            def __getitem__(self, idx):
                return ap_cache[(self.nm,) + tuple(idx)]
        w_a_in, w_a_out, w_b_in, w_b_out = _W('a_in'), _W('a_out'), _W('b_in'), _W('b_out')
        w_c_in, w_c_out, w_q, w_kv, w_o, w_gu, w_dn = _W('c_in'), _W('c_out'), _W('q'), _W('kv'), _W('o'), _W('gu'), _W('dn')
        pro_seq, tile_seq = plan_weights()
        wq_list.extend(pro_seq)
        for _ in range(NT):
            wq_list.extend(tile_seq)

        pending_pe = []
        marks = []

        def mark(lbl):
            marks.append((cx.ecnt['pe'], lbl))

        def flush_pe():
            items = list(pending_pe)
            pending_pe.clear()
            for f in items:
                f()

        def stat_mm(b, rhs_ap, rhs_res, first, last, ntok=T):
            cx.emit('pe', lambda e: e.matmul(ps[b][:, :ntok], ones[:, :], rhs_ap, start=first, stop=last),
                    reads=[rhs_res, 'consts'], writes=[('ps', b)])

        def recip(buf, bufn, ntok=T, scr=2):
            cx.emit('dve', lambda e: e.reciprocal(out=buf[:, :ntok], in_=buf[:, :ntok]), reads=[bufn], writes=[bufn])

        def finish_rstd(b, ntok=T, dst=None, dstn='rstd'):
            dst = rstd if dst is None else dst
            cx.emit('act', lambda e: e.activation(out=dst[:, :ntok], in_=ps[b][:, :ntok], func=AF.Sqrt, bias=EPS),
                    reads=[('ps', b)], writes=[dstn])
            recip(dst, dstn, ntok)

        def rms_stats(src, srcn, ntok=T):
            b = bank()
            for c in range(DC):
                q = sq[c % 4]
                cx.emit('act', lambda e, c=c, q=q: e.activation(out=q[:, :ntok], in_=src[:, c * ntok:(c + 1) * ntok],
                                                                func=AF.Square, scale=RSD),
                        reads=[(srcn, c)], writes=[('sq', c % 4)])
                stat_mm(b, q[:, :ntok], ('sq', c % 4), c == 0, c == DC - 1, ntok)
            finish_rstd(b, ntok)

        def rms_apply(src, srcn, vid, ntok=T):
            for c in range(DC):
                cx.emit('dve', lambda e, c=c: e.scalar_tensor_tensor(
                    out=h[:, c * ntok:(c + 1) * ntok], in0=src[:, c * ntok:(c + 1) * ntok], scalar=vcol(vid, c),
                    in1=rstd[:, :ntok], op0=ALU.mult, op1=ALU.mult),
                    reads=[(srcn, c), 'rstd', 'consts'], writes=[('h', c)])

        def rms_pre2(vid, need_r2=False, need_col=False):
            for c in range(DC):
                cx.emit('act', lambda e, c=c: e.activation(out=h[:, c * T:(c + 1) * T], in_=xs[:, c * T:(c + 1) * T],
                                                           func=AF.Identity, scale=vcol(vid, c)),
                        reads=[('xs', c), 'consts'], writes=[('h', c)])
            b = bank()
            held.add(b)
            for c in range(DC):
                cx.emit('act', lambda e, c=c: e.activation(out=big[:, c * T:(c + 1) * T], in_=xs[:, c * T:(c + 1) * T],
                                                           func=AF.Square, scale=RSD),
                        reads=[('xs', c)], writes=[('big', c)])
            for c in range(DC):
                pending_pe.append(lambda c=c: stat_mm(b, big[:, c * T:(c + 1) * T], ('big', c), c == 0, c == DC - 1))

            def fin():
                finish_rstd(b, T, rin, 'rin')
                held.discard(b)
                if need_r2:
                    cx.emit('dve', lambda e: e.tensor_tensor(out=rden[:, :], in0=rin[:, :], in1=rin[:, :], op=ALU.mult),
                            reads=['rin'], writes=['rden'])
                if need_col:
                    bb = bank()
                    for tc in range(4):
                        cx.emit('pe', lambda e, tc=tc: e.matmul(ps[bb][:, tc:tc + 1], rin[:, tc * 128:(tc + 1) * 128], onesf[:, 0:1],
                                                                start=True, stop=True),
                                reads=['rin', 'consts'], writes=[('ps', bb)])
                    cx.emit('act', lambda e: e.activation(out=rcol[:, 0:4], in_=ps[bb][:, 0:4], func=AF.Copy),
                            reads=[('ps', bb)], writes=['rcol'])
            pending_pe.append(fin)

        def mul_rin(out_ap, out_res, b, src=None, srcn='rin'):
            src = rin if src is None else src
            cx.emit('dve', lambda e: e.tensor_tensor(out=out_ap, in0=ps[b][:, :], in1=src[:, :], op=ALU.mult),
                    reads=[('ps', b), srcn], writes=[out_res])

        def proj_fm(src, srcn, ntok, wtiles, kc, evac, nj=4):
            ncol = nj * 128
            for ng, kts in enumerate(wtiles):
                banks = [bank() for _ in range(nj)]
                held.update(banks)
                nkt = len(kts)
                for kt, ap in enumerate(kts):
                    s = w_get(ap)
                    for j in range(nj):
                        for k in range(kc):
                            kk = kt * kc + k
                            cx.emit('pe', lambda e, s=s, j=j, k=k, kk=kk, b=banks[j]: e.matmul(
                                ps[b][:, :ntok], wbuf[s][:, k * ncol + j * 128: k * ncol + (j + 1) * 128],
                                src[:, kk * ntok:(kk + 1) * ntok],
                                start=(kt == 0 and k == 0), stop=(kt == nkt - 1 and k == kc - 1)),
                                reads=[('w', s), (srcn, kk)], writes=[('ps', banks[j])])
                    w_done()
                flush_pe()
                for j in range(nj):
                    evac(ng * nj + j, banks[j])
                for f in late_evacs:
                    f()
                late_evacs.clear()
                for j in range(nj):
                    held.discard(banks[j])
            flush_pe()

        stat_bank = [0]

        def evac_y_begin():
            stat_bank[0] = bank()
            held.add(stat_bank[0])

        post_vid = [0]

        late_evacs = []

        def evac_y(c, b):
            q = sq[c % 4]
            cx.emit('act', lambda e: e.activation(out=q[:, :], in_=ps[b][:, :], func=AF.Square, scale=RSD),
                    reads=[('ps', b)], writes=[('sq', c % 4)])
            vid = post_vid[0]
            late_evacs.append(lambda: cx.emit('act', lambda e: e.activation(
                out=y[:, c * T:(c + 1) * T], in_=ps[b][:, :], func=AF.Identity, scale=vcol(vid, c)),
                reads=[('ps', b), 'consts'], writes=[('y', c)]))
            sbk = stat_bank[0]
            pending_pe.append(lambda: stat_mm(sbk, q[:, :], ('sq', c % 4), c == 0, c == DC - 1))

        def post_norm(vid):
            flush_pe()
            finish_rstd(stat_bank[0])
            held.discard(stat_bank[0])
            assert vid == post_vid[0]
            if NWARM:
                bw = bank()
                for _ in range(NWARM):
                    cx.emit('pe', lambda e: e.matmul(ps[bw][:, :], ones[:, :], wsT[:, 0:512], start=True, stop=True),
                            reads=['consts'], writes=[('ps', bw)])
            for c in range(DC):
                en = 'dve'
                cx.emit(en, lambda e, c=c: e.tensor_tensor(
                    out=y[:, c * T:(c + 1) * T], in0=y[:, c * T:(c + 1) * T], in1=rstd[:, :], op=ALU.mult),
                    reads=[('y', c), 'rstd'], writes=[('y', c)])
                cx.emit(en, lambda e, c=c: e.tensor_tensor(
                    out=xs[:, c * T:(c + 1) * T], in0=xs[:, c * T:(c + 1) * T], in1=y[:, c * T:(c + 1) * T], op=ALU.add),
                    reads=[('xs', c), ('y', c)], writes=[('xs', c)])

        def out_proj(wt, nm, i_slot, vid):
            mark('outproj_' + nm)
            post_vid[0] = vid
            evac_y_begin()
            proj_fm(big, 'big', T, [wt[i_slot, n] for n in range(4)], 8, evac_y)
            post_norm(vid)

        def conv_chunk(cb, cbn, K, vid_w, chunk, halo_ap, halo_res, first_bias_vid=None, out_ap=None, out_res=None):
            H = K - 1
            cx.emit('dve', lambda e: e.tensor_copy(out=cb[:, 0:H], in_=halo_ap), reads=[halo_res], writes=[cbn])
            cx.emit('dve', lambda e: e.tensor_copy(out=halo_ap, in_=cb[:, T:T + H]), reads=[cbn], writes=[halo_res])
            if first_bias_vid is None:
                cx.emit('dve', lambda e: e.tensor_scalar(out=out_ap, in0=cb[:, H:H + T], scalar1=vcol(vid_w(K - 1), chunk),
                                                         scalar2=None, op0=ALU.mult),
                        reads=[cbn, 'consts'], writes=[out_res])
            else:
                cx.emit('dve', lambda e: e.tensor_scalar(out=out_ap, in0=cb[:, H:H + T], scalar1=vcol(vid_w(K - 1), chunk),
                                                         scalar2=vcol(first_bias_vid, chunk), op0=ALU.mult, op1=ALU.add),
                        reads=[cbn, 'consts'], writes=[out_res])
            for k in range(K - 2, -1, -1):
                cx.emit('dve', lambda e, k=k: e.scalar_tensor_tensor(
                    out=out_ap, in0=cb[:, k:k + T], scalar=vcol(vid_w(k), chunk), in1=out_ap,
                    op0=ALU.mult, op1=ALU.add),
                    reads=[cbn, out_res, 'consts'], writes=[out_res])

        def mixer_a(i):
            mark('A_in')
            slot = i // 3
            rms_pre2(V_MIX(i, 0), need_r2=True)
            for g in range(4):
                def ev_b(c, b):
                    j = c % 4
                    mul_rin(gt[:, j * T:(j + 1) * T], ('gt', j), b)

                def ev_c(c, b):
                    j = c % 4
                    mul_rin(y[:, j * T:(j + 1) * T], ('y', j), b, rden, 'rden')

                def ev_z(c, b, g=g):
                    j = c % 4
                    chunk = g * 4 + j
                    cb = cbuf[chunk % 2]
                    cbn = ('cbuf', chunk % 2)
                    cx.emit('dve', lambda e: e.tensor_tensor(out=cb[:, 2:2 + T], in0=y[:, j * T:(j + 1) * T],
                                                             in1=ps[b][:, :], op=ALU.mult),
                            reads=[('y', j), ('ps', b)], writes=[cbn])
                    hoff = (slot * DC + chunk) * 2
                    t_ = tmp[chunk % 3]
                    conv_chunk(cb, cbn, 3, lambda k: V_ACONV(slot, k), chunk, haloA[:, hoff:hoff + 2], ('haloA', slot, chunk),
                               out_ap=t_[:, :], out_res=('tmp', chunk % 3))
                    cx.emit('dve', lambda e: e.tensor_tensor(out=big[:, chunk * T:(chunk + 1) * T], in0=t_[:, :],
                                                             in1=gt[:, j * T:(j + 1) * T], op=ALU.mult),
                            reads=[('tmp', chunk % 3), ('gt', j)], writes=[('big', chunk)])
                proj_fm(h, 'h', T, [w_a_in[slot, g]], 8, ev_b)
                proj_fm(h, 'h', T, [w_a_in[slot, 4 + g]], 8, ev_c)
                proj_fm(h, 'h', T, [w_a_in[slot, 8 + g]], 8, ev_z)
            out_proj(w_a_out, 'a_out', slot, V_MIX(i, 1))

        def mixer_c(i):
            mark('C_in')
            slot = i // 3
            rms_pre2(V_MIX(i, 0))
            b1, b2 = bank(), bank()
            held.add(b1)
            held.add(b2)
            DG0 = 16

            def dg_load(c):
                s3 = c % 3
                base = (DG0 + 8 * s3) * T
                cx.dma('sp', 'dg%d' % s3, lambda e: e.dma_start(out=big[:, base:base + 3968], in_=dgc[c]),
                       reads=[('dgc', c)], writes=[('big', DG0 + 8 * s3 + n) for n in range(8)])
            for c in range(3):
                dg_load(c)
            prev_stat = [None]

            def run_prev_stat():
                if prev_stat[0] is not None:
                    prev_stat[0]()
                    prev_stat[0] = None

            for g in range(4):
                def ev_g(c, b):
                    j = c % 4
                    mul_rin(gt[:, j * T:(j + 1) * T], ('gt', j), b)
                    cx.emit('act', lambda e: e.activation(out=gt[:, j * T:(j + 1) * T], in_=gt[:, j * T:(j + 1) * T], func=AF.Sigmoid),
                            reads=[('gt', j)], writes=[('gt', j)])

                def ev_a(c, b, g=g):
                    j = c % 4
                    chunk = g * 4 + j
                    off = j * 544
                    cbn = ('cbb', j)
                    t_ = tmp[j % 3]
                    mul_rin(t_[:, :], ('tmp', j % 3), b)
                    cx.emit('dve', lambda e: e.tensor_tensor(out=pTT[:, off + 30: off + 30 + T], in0=gt[:, j * T:(j + 1) * T],
                                                             in1=t_[:, :], op=ALU.mult),
                            reads=[('gt', j), ('tmp', j % 3)], writes=[cbn])
                    hres = ('haloC', chunk)
                    cx.emit('dve', lambda e: e.tensor_copy(out=pTT[:, off: off + 30], in_=haloC[:, chunk * 30:(chunk + 1) * 30]),
                            reads=[hres], writes=[cbn])
                    cx.emit('dve', lambda e: e.tensor_copy(out=haloC[:, chunk * 30:(chunk + 1) * 30], in_=pTT[:, off + T: off + T + 30]),
                            reads=[cbn], writes=[hres])

                    def conv():
                        s3 = chunk % 3
                        base = (DG0 + 8 * s3) * T
                        dgres = [('big', DG0 + 8 * s3 + n) for n in range(8)]
                        bc = bank()
                        for k in range(31):
                            cx.emit('pe', lambda e, k=k: e.matmul(ps[bc][:, :], big[:, base + k * 128: base + (k + 1) * 128],
                                                                  pTT[:, off + k: off + k + T], start=(k == 0), stop=(k == 30)),
                                    reads=dgres + [cbn], writes=[('ps', bc)])
                        if chunk + 3 < DC:
                            dg_load(chunk + 3)
                        run_prev_stat()
                        q1, q2 = sq[(2 * chunk) % 4], sq[(2 * chunk + 1) % 4]
                        cx.emit('act', lambda e: e.activation(out=y[:, chunk * T:(chunk + 1) * T], in_=ps[bc][:, :], func=AF.Identity,
                                                              bias=vcol(V_CB, chunk)),
                                reads=[('ps', bc), 'consts'], writes=[('y', chunk)])
                        cx.emit('act', lambda e: e.activation(out=q1[:, :], in_=ps[bc][:, :], func=AF.Identity,
                                                              bias=vcol(V_CB, chunk)),
                                reads=[('ps', bc), 'consts'], writes=[('sq', (2 * chunk) % 4)])
                        cx.emit('act', lambda e: e.activation(out=q2[:, :], in_=ps[bc][:, :], func=AF.Square, scale=RSD,
                                                              bias=cbR[:, chunk:chunk + 1]),
                                reads=[('ps', bc), 'consts'], writes=[('sq', (2 * chunk + 1) % 4)])

                        def stat():
                            stat_mm(b1, q1[:, :], ('sq', (2 * chunk) % 4), chunk == 0, chunk == DC - 1)
                            stat_mm(b2, q2[:, :], ('sq', (2 * chunk + 1) % 4), chunk == 0, chunk == DC - 1)
                        prev_stat[0] = stat
                    pending_pe.append(conv)
                proj_fm(h, 'h', T, [w_c_in[slot, 4 + g]], 8, ev_g)
                proj_fm(h, 'h', T, [w_c_in[slot, g]], 8, ev_a)
            flush_pe()
            run_prev_stat()
            cx.emit('act', lambda e: e.activation(out=meanb[:, :], in_=ps[b1][:, :], func=AF.Copy, scale=1.0 / D),
                    reads=[('ps', b1)], writes=['rin'])
            cx.emit('dve', lambda e: e.tensor_tensor(out=rden[:, :], in0=meanb[:, :], in1=meanb[:, :], op=ALU.mult),
                    reads=['rin'], writes=['rden'])
            cx.emit('dve', lambda e: e.tensor_tensor(out=rstd[:, :], in0=ps[b2][:, :], in1=rden[:, :], op=ALU.subtract),
                    reads=[('ps', b2), 'rden'], writes=['rstd'])
            held.discard(b1)
            held.discard(b2)
            cx.emit('act', lambda e: e.activation(out=rstd[:, :], in_=rstd[:, :], func=AF.Sqrt, bias=EPS),
                    reads=['rstd'], writes=['rstd'])
            cx.emit('dve', lambda e: e.reciprocal(out=rstd[:, :], in_=rstd[:, :]), reads=['rstd'], writes=['rstd'])
            for c in range(DC):
                t_ = tmp[c % 3]
                cx.emit('dve', lambda e, c=c, t_=t_: e.tensor_tensor(out=t_[:, :], in0=y[:, c * T:(c + 1) * T],
                                                                     in1=meanb[:, :], op=ALU.subtract),
                        reads=[('y', c), 'rin'], writes=[('tmp', c % 3)])
                cx.emit('dve', lambda e, t_=t_: e.tensor_tensor(out=t_[:, :], in0=t_[:, :], in1=rstd[:, :], op=ALU.mult),
                        reads=[('tmp', c % 3), 'rstd'], writes=[('tmp', c % 3)])
                cx.emit('act', lambda e, c=c, t_=t_: e.activation(out=big[:, c * T:(c + 1) * T], in_=t_[:, :], func=AF.Silu,
                                                                  scale=vcol(V_CLG, c), bias=vcol(V_CLB, c)),
                        reads=[('tmp', c % 3), 'consts'], writes=[('big', c)])
            out_proj(w_c_out, 'c_out', slot, V_MIX(i, 1))

        VH = DC * T

        def mixer_b(i):
            mark('B_in')
            slot = i // 3
            rms_pre2(V_MIX(i, 0), need_col=True)

            def ev_u(c, b):
                t_ = tmp[c % 2]
                mul_rin(t_[:, :], ('tmp', c % 2), b)
                cx.emit('act', lambda e: e.activation(out=big[:, c * T:(c + 1) * T], in_=t_[:, :], func=AF.Gelu_apprx_tanh),
                        reads=[('tmp', c % 2)], writes=[('big', c)])
            proj_fm(h, 'h', T, [w_b_in[slot, n] for n in range(4)], 8, ev_u)
            for n in range(4):
                banks = [bank() for _ in range(4)]
                for kt, ap in enumerate(w_b_in[slot, 4 + n]):
                    s = w_get(ap)
                    for tc in range(4):
                        for k in range(8):
                            kk = kt * 8 + k
                            cx.emit('pe', lambda e, s=s, tc=tc, k=k, kk=kk, b=banks[tc]: e.matmul(
                                ps[b][:, :], h[:, kk * T + tc * 128: kk * T + (tc + 1) * 128], wbuf[s][:, k * 512:(k + 1) * 512],
                                start=(kk == 0), stop=(kk == DC - 1)),
                                reads=[('w', s), ('h', kk)], writes=[('ps', banks[tc])])
                    w_done()
                for tc in range(4):
                    cx.emit('act', lambda e, tc=tc, n=n, b=banks[tc]: e.activation(
                        out=y[:, tc * 2048 + n * 512: tc * 2048 + (n + 1) * 512], in_=ps[b][:, :], func=AF.Gelu_apprx_tanh,
                        scale=rcol[:, tc:tc + 1]),
                        reads=[('ps', banks[tc]), 'rcol'], writes=[('y', tc * 4 + n)])
            for tc in range(4):
                for n in range(4):
                    cx.emit('dve', lambda e, tc=tc, n=n: e.bn_stats(out=bst[:, (tc * 4 + n) * 6:(tc * 4 + n + 1) * 6],
                                                                    in_=y[:, tc * 2048 + n * 512: tc * 2048 + (n + 1) * 512]),
                            reads=[('y', tc * 4 + n)], writes=[('bst', tc * 4 + n)])
                cx.emit('dve', lambda e, tc=tc: e.bn_aggr(out=mv[:, tc * 2:tc * 2 + 2], in_=bst[:, tc * 24:(tc + 1) * 24]),
                        reads=[('bst', tc * 4 + n) for n in range(4)], writes=[('mv', tc)])
                cx.emit('act', lambda e, tc=tc: e.activation(out=rs[:, tc:tc + 1], in_=mv[:, tc * 2 + 1:tc * 2 + 2], func=AF.Sqrt, bias=EPS),
                        reads=[('mv', tc)], writes=[('rs', tc)])
                cx.emit('dve', lambda e, tc=tc: e.reciprocal(out=rs[:, tc:tc + 1], in_=rs[:, tc:tc + 1]),
                        reads=[('rs', tc)], writes=[('rs', tc)])
                cx.emit('dve', lambda e, tc=tc: e.tensor_scalar(
                    out=big[:, VH + tc * 2048: VH + (tc + 1) * 2048], in0=y[:, tc * 2048:(tc + 1) * 2048],
                    scalar1=mv[:, tc * 2:tc * 2 + 1], scalar2=rs[:, tc:tc + 1], op0=ALU.subtract, op1=ALU.mult),
                    reads=[('y', tc * 4 + n) for n in range(4)] + [('mv', tc), ('rs', tc)],
                    writes=[('big', DC + tc * 4 + n) for n in range(4)])
            for c in range(DC):
                g = c // 2
                b = bank()
                for n in range(4):
                    cx.emit('pe', lambda e, n=n, c=c, g=g, b=b: e.matmul(
                        ps[b][:, n * 128:(n + 1) * 128], big[:, VH + n * 2048 + c * 128: VH + n * 2048 + (c + 1) * 128],
                        wsT[:, g * 128:(g + 1) * 128], start=True, stop=True),
                        reads=[('big', DC + n * 4 + c // 4), 'consts'], writes=[('ps', b)])
                t0, t1 = tmp[0], tmp[1]
                cx.emit('dve', lambda e, c=c, g=g: e.scalar_tensor_tensor(
                    out=t0[:, 0:128], in0=Rb[:, g * 128:(g + 1) * 128], scalar=vcol(V_BVB, c),
                    in1=sbias[:, g * 128:(g + 1) * 128], op0=ALU.mult, op1=ALU.add),
                    reads=['consts'], writes=[('tmp', 0)])
                for n in range(4):
                    cx.emit('dve', lambda e, c=c, n=n, b=b: e.scalar_tensor_tensor(
                        out=t1[:, n * 128:(n + 1) * 128], in0=ps[b][:, n * 128:(n + 1) * 128], scalar=vcol(V_BVG, c),
                        in1=t0[:, 0:128], op0=ALU.mult, op1=ALU.add),
                        reads=[('ps', b), ('tmp', 0), 'consts'], writes=[('tmp', 1)])
                cx.emit('dve', lambda e, c=c: e.tensor_tensor(out=big[:, c * T:(c + 1) * T], in0=big[:, c * T:(c + 1) * T],
                                                              in1=t1[:, :], op=ALU.mult),
                        reads=[('big', c), ('tmp', 1)], writes=[('big', c)])
            out_proj(w_b_out, 'b_out', slot, V_MIX(i, 1))

        KT0 = DC * T
        V0 = DC * T + 4096
        SCALE = 512.0 ** -0.5

        def xattn(i):
            mark('xattn_q')
            cx.dma('sp', 'kvld', lambda e: e.dma_start(out=big[:, KT0:KT0 + 8192], in_=kvc[i]),
                   reads=[('kvc', i)], writes=[('big', DC + n) for n in range(16)])
            rms_pre2(V_XA(i, 0))

            def ev_q(c, b):
                mul_rin(big[:, c * T:(c + 1) * T], ('big', c), b)
            proj_fm(h, 'h', T, [w_q[i, n] for n in range(4)], 8, ev_q)
            kvres = [('big', DC + n) for n in range(16)]
            mark('xattn_core')
            for hd in range(4):
                po = (hd % 2) * 1024
                pn = ('pT', hd % 2)
                for mc in range(2):
                    b = bank()
                    for j in range(4):
                        ch = hd * 4 + j
                        cx.emit('pe', lambda e, ch=ch, mc=mc, j=j, b=b: e.matmul(
                            ps[b][:, :], big[:, KT0 + ch * 256 + mc * 128: KT0 + ch * 256 + (mc + 1) * 128],
                            big[:, ch * T:(ch + 1) * T], start=(j == 0), stop=(j == 3)),
                            reads=kvres + [('big', ch)], writes=[('ps', b)])
                    cx.emit('act', lambda e, mc=mc, b=b, po=po: e.activation(out=pTT[:, po + mc * T: po + (mc + 1) * T], in_=ps[b][:, :],
                                                                           func=AF.Exp, scale=SCALE),
                            reads=[('ps', b)], writes=[(pn, mc)])
                bd = bank()
                for mc in range(2):
                    cx.emit('pe', lambda e, mc=mc, po=po: e.matmul(ps[bd][:, :], ones[:, :], pTT[:, po + mc * T: po + (mc + 1) * T],
                                                                   start=(mc == 0), stop=(mc == 1)),
                            reads=[(pn, mc), 'consts'], writes=[('ps', bd)])
                cx.emit('dve', lambda e: e.reciprocal(out=rden[:, :], in_=ps[bd][:, :]), reads=[('ps', bd)], writes=['rden'])
                for j in range(4):
                    ch = hd * 4 + j
                    b = bank()
                    for mc in range(2):
                        cx.emit('pe', lambda e, ch=ch, mc=mc, b=b, po=po: e.matmul(
                            ps[b][:, :], big[:, V0 + mc * 2048 + ch * 128: V0 + mc * 2048 + (ch + 1) * 128],
                            pTT[:, po + mc * T: po + (mc + 1) * T], start=(mc == 0), stop=(mc == 1)),
                            reads=kvres + [(pn, mc)], writes=[('ps', b)])
                    cx.emit('dve', lambda e, ch=ch, b=b: e.tensor_tensor(out=big[:, ch * T:(ch + 1) * T], in0=ps[b][:, :],
                                                                        in1=rden[:, :], op=ALU.mult),
                            reads=[('ps', b), 'rden'], writes=[('big', ch)])
            out_proj(w_o, 'o', i, V_XA(i, 1))

        def ffn(i):
            mark('ffn_gu')
            rms_pre2(V_FFN(i, 0))
            for grp in range(22):
                def ev_gu(c, b, grp=grp):
                    j = c % 4
                    if j < 2:
                        t_ = tmp[j]
                        mul_rin(t_[:, :], ('tmp', j), b)
                        cx.emit('act', lambda e: e.activation(out=t_[:, :], in_=t_[:, :], func=AF.Silu),
                                reads=[('tmp', j)], writes=[('tmp', j)])
                    else:
                        t_ = tmp[j - 2]
                        hc = grp * 2 + (j - 2)
                        mul_rin(gt[:, (j - 2) * T:(j - 1) * T], ('gt', j - 2), b)
                        cx.emit('dve', lambda e: e.tensor_tensor(out=big[:, hc * T:(hc + 1) * T], in0=t_[:, :],
                                                                 in1=gt[:, (j - 2) * T:(j - 1) * T], op=ALU.mult),
                                reads=[('tmp', j - 2), ('gt', j - 2)], writes=[('big', hc)])
                proj_fm(h, 'h', T, [w_gu[i, grp]], 8, ev_gu)
            mark('ffn_down')
            post_vid[0] = V_FFN(i, 1)
            evac_y_begin()
            proj_fm(big, 'big', T, [w_dn[i, n] for n in range(8)], 11, evac_y, nj=2)
            post_norm(V_FFN(i, 1))

        def load_x(t):
            for c in range(DC):
                cx.dma('sp', 'xld%d' % c, lambda e, c=c: e.dma_start(out=xs[:, c * T:(c + 1) * T], in_=xT[t, :, c * T:(c + 1) * T]),
                       writes=[('xs', c)])

        load_x(0)
        cx.dma('sp', 'cst', lambda e: e.dma_start(out=vecs[:, :], in_=vecs_d), writes=['consts'])
        cx.dma('sp', 'cst', lambda e: e.dma_start(out=sbias[:, :], in_=sbias_d), writes=['consts'])
        cx.dma('sp', 'cst', lambda e: e.dma_start(out=y[:, 0:1024], in_=wsT_d), writes=['consts', ('y', 0), ('y', 1)])
        cx.emit('dve', lambda e: e.memset(ones[:, :], 1.0), writes=['consts'])
        cx.emit('dve', lambda e: e.memset(onesf[:, :], 1.0 / 128.0), writes=['consts'])
        cx.emit('dve', lambda e: e.memset(haloA[:, :], 0.0), writes=[('haloA', s_, c) for s_ in range(2) for c in range(DC)])
        cx.emit('dve', lambda e: e.memset(haloC[:, :], 0.0), writes=[('haloC', c) for c in range(DC)])
        if 1 in layers:
            cx.emit('pool', lambda e: e.affine_select(out=wsT[:, :], in_=y[:, 0:1024], pattern=[[0, 8], [1, 128]],
                                                       compare_op=ALU.is_ge, fill=0.0, base=0, channel_multiplier=-1),
                    reads=['consts', ('y', 0), ('y', 1)], writes=['consts'])
            for hf in range(2):
                b = bank()
                for g4 in range(4):
                    g = hf * 4 + g4
                    cx.emit('pe', lambda e, g=g, g4=g4, b=b: e.matmul(ps[b][:, g4 * 128:(g4 + 1) * 128], ones[:, :],
                                                                      wsT[:, g * 128:(g + 1) * 128], start=True, stop=True),
                            reads=['consts'], writes=[('ps', b)])
                cx.emit('act', lambda e, hf=hf, b=b: e.activation(out=Rb[:, hf * 512:(hf + 1) * 512], in_=ps[b][:, :], func=AF.Copy),
                        reads=[('ps', b)], writes=['consts'])
        def dg_setup():
            cx.emit('dve', lambda e: e.memset(identb[:, :], 1.0), writes=['identb'])
            cx.emit('pool', lambda e: e.affine_select(out=identb[:, :], in_=identb[:, :], pattern=[[1, 128]],
                                                       compare_op=ALU.is_equal, fill=0.0, base=0, channel_multiplier=-1),
                    reads=['identb'], writes=['identb'])
            cx.emit('dve', lambda e: e.tensor_scalar(out=cbR[:, 0:DC], in0=vecs[:, V_CB * DC:(V_CB + 1) * DC], scalar1=RSD,
                                                     scalar2=None, op0=ALU.mult),
                    reads=['consts'], writes=['consts'])

        def dg_build(c):
            sl = c % 2
            base = (28 + sl * 8) * T
            for k in range(31):
                cx.emit('dve', lambda e, k=k: e.tensor_scalar(
                    out=big[:, base + k * 128: base + (k + 1) * 128], in0=identb[:, :], scalar1=vcol(V_CCONV(k), c),
                    scalar2=None, op0=ALU.mult),
                    reads=['identb', 'consts'], writes=[('big', 28 + sl * 8 + k // 4)])
            cx.dma('sp', 'dgst', lambda e: e.dma_start(out=dgc[c], in_=big[:, base:base + 3968]),
                   reads=[('big', 28 + sl * 8 + n) for n in range(8)], writes=[('dgc', c)])

        dg_todo = list(range(DC)) if 2 in layers else []
        if dg_todo:
            dg_setup()
        cx.dma('sp', 'mld', lambda e: e.dma_start(out=y[:, 0:DC * NMEM], in_=memT),
               writes=[('y', c) for c in range(DC)])
        rms_stats(y, 'y', ntok=NMEM)
        for i in layers:
            rms_apply(y, 'y', V_XA(i, 2), ntok=NMEM)
            nshare = (len(dg_todo) + (len(layers) - layers.index(i)) - 1) // (len(layers) - layers.index(i))
            for _ in range(nshare):
                dg_build(dg_todo.pop(0))

            def ev_k(c, b):
                cx.emit('act', lambda e: e.activation(out=big[:, c * 256:(c + 1) * 256], in_=ps[b][:, :256], func=AF.Copy),
                        reads=[('ps', b)], writes=[('big', c)])
            proj_fm(h, 'h', NMEM, [w_kv[i, n] for n in range(4)], 8, ev_k)
            for n in range(4):
                banks = [bank() for _ in range(2)]
                for kt, ap in enumerate(w_kv[i, 4 + n]):
                    s = w_get(ap)
                    for tc in range(2):
                        for k in range(8):
                            kk = kt * 8 + k
                            cx.emit('pe', lambda e, s=s, tc=tc, k=k, kk=kk, b=banks[tc]: e.matmul(
                                ps[b][:, :], h[:, kk * NMEM + tc * 128: kk * NMEM + (tc + 1) * 128], wbuf[s][:, k * 512:(k + 1) * 512],
                                start=(kk == 0), stop=(kk == DC - 1)),
                                reads=[('w', s), ('h', kk)], writes=[('ps', banks[tc])])
                    w_done()
                for tc in range(2):
                    cx.emit('act', lambda e, tc=tc, n=n, b=banks[tc]: e.activation(
                        out=big[:, 4096 + tc * 2048 + n * 512: 4096 + tc * 2048 + (n + 1) * 512], in_=ps[b][:, :], func=AF.Copy),
                        reads=[('ps', banks[tc])], writes=[('big', 16 + tc * 4 + n)])
            cx.dma('sp', 'kvst', lambda e, i=i: e.dma_start(out=kvc[i], in_=big[:, 0:8192]),
                   reads=[('big', c) for c in range(24)], writes=[('kvc', i)])

        assert not dg_todo
        if 2 in layers:
            for c in range(DC):
                cx.last_w[('dgc', c)] = cx.last_w[('dgc', DC - 1)]
        for t in range(NT):
            if t > 0:
                load_x(t)
            for i in layers:
                [mixer_a, mixer_b, mixer_c][i % 3](i)
                if stop == 'mixer':
                    break
                xattn(i)
                if stop == 'xattn':
                    break
                ffn(i)
            for c in range(DC):
                cx.dma('sp', 'ost%d' % c, lambda e, t=t, c=c: e.dma_start(out=outT[t, :, c * T:(c + 1) * T], in_=xs[:, c * T:(c + 1) * T]),
                       reads=[('xs', c)], writes=[('out', t, c)])
        if dbg:
            allres = list(cx.last_w.keys())
            for (nm, buf, cols, dt) in [('h', h, DC * T, BF16), ('y', y, DC * T, F32), ('big', big, FC * T, BF16),
                                        ('rstd', rstd, T, F32), ('vecs', vecs, NV * DC, F32), ('gt', gt, 4 * T, F32)]:
                dd = nc.dram_tensor("dbg_" + nm, [128, cols], dt, kind="ExternalOutput").ap()
                cx.dma('sp', 'ost', lambda e, dd=dd, buf=buf: e.dma_start(out=dd, in_=buf[:, :]), reads=allres, writes=[('out', 'dbg' + nm)])
            cx.wait_all('sp', [('out', 'dbg' + nm) for nm in ['h', 'y', 'big', 'rstd', 'vecs', 'gt']])
        cx.wait_all('sp', [('out', t, c) for t in range(NT) for c in range(DC)])
        assert stop or wstate['use'] == len(wq_list), (wstate, len(wq_list))
        print("sbuf bytes remaining", nc.sbuf_bytes_remaining)
        print("instructions", cx.n_inst, "waits", cx.n_wait, "pe", cx.ecnt['pe'])
    return nc


def tile_w(w, kc, ncol=512):
    K, N = w.shape
    nkt = K // (128 * kc)
    ng = N // ncol
    a = w.reshape(nkt, kc, 128, ng, ncol).transpose(3, 0, 2, 1, 4)
    return np.ascontiguousarray(a).reshape(ng, nkt, 128, kc * ncol)


def prep_shared(inp):
    f = lambda a: np.asarray(a, dtype=np.float32)
    sh = {}
    rows = [f(inp['mix_norm']).reshape(8, D), f(inp['xa_norm']).reshape(12, D), f(inp['ffn_norm']).reshape(8, D),
            f(inp['a_conv_w']).reshape(6, D), f(inp['c_conv_w']).reshape(31, D), f(inp['c_conv_b']).reshape(1, D),
            f(inp['c_ln_g']).reshape(1, D), f(inp['c_ln_b']).reshape(1, D), f(inp['b_v_g']).reshape(1, D),
            f(inp['b_v_b']).reshape(1, D)]
    allv = np.concatenate(rows, axis=0)
    assert allv.shape[0] == NV
    sh['vecs'] = np.ascontiguousarray(allv.reshape(NV, DC, 128).transpose(2, 0, 1)).reshape(128, NV * DC)
    ws = f(inp['b_w_s'])[0]
    sh['wsT'] = np.ascontiguousarray(ws.transpose(2, 0, 1)).reshape(128, 8 * 128)
    sb_ = f(inp['b_s_bias'])[0].reshape(1, 8 * 128)
    sh['sbias'] = np.ascontiguousarray(np.broadcast_to(sb_, (128, 1024)))

    def tw(name, kc=8):
        w = f(inp[name])
        return np.stack([tile_w(w[l], kc) for l in range(w.shape[0])])
    for name in ['a_w_in', 'a_w_out', 'b_w_in', 'b_w_out', 'c_w_in', 'c_w_out', 'xa_wq', 'xa_wkv', 'xa_wo']:
        sh[name] = tw(name)
    gu = f(inp['ffn_w_gu'])
    gl = []
    for l in range(gu.shape[0]):
        gate = gu[l][:, :DFF].reshape(D, 22, 256)
        up = gu[l][:, DFF:].reshape(D, 22, 256)
        wi = np.concatenate([gate, up], axis=2).reshape(D, 2 * DFF)
        gl.append(tile_w(wi, 8))
    sh['ffn_w_gu'] = np.stack(gl)
    dn = f(inp['ffn_w_down'])
    sh['ffn_w_down'] = np.stack([tile_w(dn[l], 11, 256) for l in range(dn.shape[0])])
    return sh


def prep_core(x_b, mem_b, NT):
    xt = x_b[:NT * T].reshape(NT, T, DC, 128).transpose(0, 3, 2, 1)
    xt = np.ascontiguousarray(xt).reshape(NT, 128, DC * T)
    mt = np.ascontiguousarray(mem_b.reshape(NMEM, DC, 128).transpose(2, 1, 0)).reshape(128, DC * NMEM)
    return {'xT': xt, 'memT': mt}


def unprep_out(o, NT):
    return np.ascontiguousarray(o.reshape(NT, 128, DC, T).transpose(0, 3, 2, 1)).reshape(NT * T, D)


def kernel(**inputs):
    NT = SEQ // T
    x = np.asarray(inputs['x'], dtype=np.float32)
    mem = np.asarray(inputs['mem'], dtype=np.float32)
    sh = prep_shared(inputs)
    nc = build(NT, [0, 1, 2, 3])
    in_maps = []
    for b in range(8):
        m = dict(sh)
        m.update(prep_core(x[b], mem[b], NT))
        in_maps.append(m)
    res = run_bass_kernel_spmd(nc, in_maps, core_ids=list(range(8)))
    out = np.stack([unprep_out(np.asarray(res.results[b]['outT']), NT) for b in range(8)])
    return out.astype(np.float32)
```

```python
import numpy as np
from contextlib import ExitStack
import concourse.bass as bass
import concourse.mybir as mybir
from concourse.bass_utils import run_bass_kernel_spmd

F32 = mybir.dt.float32
BF16 = mybir.dt.bfloat16
AF = mybir.ActivationFunctionType
ALU = mybir.AluOpType

D = 2048
DC = 16
T = 512
SEQ = 4096
NMEM = 256
DFF = 5632
FC = 44
EPS = 1e-6
NV = 70
NSLOT = 4
NWARM = 0
WCOLS = 4096
SAME_SYNC = True
RSD = float(D) ** -0.5


class Ctx:
    def __init__(self, nc, stack):
        self.nc = nc
        self.eng = {'pe': nc.tensor, 'act': nc.scalar, 'dve': nc.vector,
                    'pool': nc.gpsimd, 'sp': nc.sync}
        self.sems = {}
        self.ecnt = {e: 0 for e in self.eng}
        self.dcnt = {}
        self.waited = {e: {} for e in self.eng}
        self.last_w = {}
        self.readers = {}
        self.stack = stack
        for e in self.eng:
            self.sems['e_' + e] = stack.enter_context(nc.semaphore('s_e_' + e))
        self.n_inst = 0
        self.n_wait = 0

    def dma_sem(self, name):
        self.sems[name] = self.stack.enter_context(self.nc.semaphore('s_' + name))
        self.dcnt[name] = 0

    def _waits(self, e, reads, writes):
        need = {}
        for r in reads:
            ev = self.last_w.get(r)
            if ev is not None and need.get(ev[0], 0) < ev[1]:
                need[ev[0]] = ev[1]
        for w in writes:
            ev = self.last_w.get(w)
            if ev is not None and need.get(ev[0], 0) < ev[1]:
                need[ev[0]] = ev[1]
            for k, v in self.readers.get(w, {}).items():
                if need.get(k, 0) < v:
                    need[k] = v
        own = 'e_' + e
        for k, v in need.items():
            if k == own and (e == 'pe' or e == 'sp' or not SAME_SYNC):
                continue
            if self.waited[e].get(k, 0) >= v:
                continue
            self.eng[e].wait_ge(self.sems[k], v)
            self.waited[e][k] = v
            self.n_wait += 1

    def _record(self, ev, reads, writes):
        k, v = ev
        for r in reads:
            d = self.readers.setdefault(r, {})
            if d.get(k, 0) < v:
                d[k] = v
        for w in writes:
            self.last_w[w] = ev
            self.readers[w] = {}

    def emit(self, e, fn, reads=(), writes=()):
        self._waits(e, reads, writes)
        inst = fn(self.eng[e])
        self.ecnt[e] += 1
        inst.then_inc(self.sems['e_' + e], 1)
        self._record(('e_' + e, self.ecnt[e]), reads, writes)
        self.n_inst += 1
        return inst

    def dma(self, q, semname, fn, reads=(), writes=()):
        self._waits(q, reads, writes)
        inst = fn(self.eng[q])
        self.dcnt[semname] += 16
        inst.then_inc(self.sems[semname], 16)
        self._record((semname, self.dcnt[semname]), reads, writes)
        self.n_inst += 1
        return inst

    def wait_all(self, e, resources):
        self._waits(e, list(resources), [])


def V_MIX(i, j): return i * 2 + j
def V_XA(i, j): return 8 + i * 3 + j
def V_FFN(i, j): return 20 + i * 2 + j
def V_ACONV(s, k): return 28 + s * 3 + k
def V_CCONV(k): return 34 + k
V_CB, V_CLG, V_CLB, V_BVG, V_BVB = 65, 66, 67, 68, 69


def build(NT, layers, stop=None, dbg=False):
    nc = bass.Bass("TRN2", target_bir_lowering=False)

    def din(name, shape, dt=F32):
        return nc.dram_tensor(name, shape, dt, kind="ExternalInput").ap()

    xT = din("xT", [NT, 128, DC * T])
    memT = din("memT", [128, DC * NMEM])
    vecs_d = din("vecs", [128, NV * DC])
    wsT_d = din("wsT", [128, 8 * 128])
    sbias_d = din("sbias", [128, 8 * 128])
    w_a_in = din("a_w_in", [2, 12, 2, 128, 4096])
    w_a_out = din("a_w_out", [2, 4, 2, 128, 4096])
    w_b_in = din("b_w_in", [1, 8, 2, 128, 4096])
    w_b_out = din("b_w_out", [1, 4, 2, 128, 4096])
    w_c_in = din("c_w_in", [1, 8, 2, 128, 4096])
    w_c_out = din("c_w_out", [1, 4, 2, 128, 4096])
    w_q = din("xa_wq", [4, 4, 2, 128, 4096])
    w_kv = din("xa_wkv", [4, 8, 2, 128, 4096])
    w_o = din("xa_wo", [4, 4, 2, 128, 4096])
    w_gu = din("ffn_w_gu", [4, 22, 2, 128, 4096])
    w_dn = din("ffn_w_down", [4, 8, 4, 128, 11 * 256])
    outT = nc.dram_tensor("outT", [NT, 128, DC * T], F32, kind="ExternalOutput").ap()
    kvc = nc.dram_tensor("kvc", [4, 128, 8192], BF16).ap()
    dgc = nc.dram_tensor("dgc", [DC, 128, 3968], BF16).ap()

    with ExitStack() as st:
        cx = Ctx(nc, st)
        for n in ['cst', 'kvst', 'kvld', 'mld', 'ost', 'dgst', 'dg0', 'dg1', 'dg2'] + ['w%d' % s for s in range(NSLOT)] + ['xld%d' % c for c in range(DC)] + ['ost%d' % c for c in range(DC)]:
            cx.dma_sem(n)

        def sb(name, cols, dt):
            return st.enter_context(nc.sbuf_tensor("sb_" + name, [128, cols], dt))

        xs = sb("xs", DC * T, F32)
        h = sb("h", DC * T, BF16)
        big = sb("big", FC * T, BF16)
        y = sb("y", DC * T, F32)
        gt = sb("gt", 4 * T, F32)
        wbuf = [sb("w%d" % s, WCOLS, BF16) for s in range(NSLOT)]
        rstd = sb("rstd", T, F32)
        rden = sb("rden", T, F32)
        rin = sb("rin", T, F32)
        meanb = rin
        onesf = sb("onesf", 1, F32)
        rcol = sb("rcol", 4, F32)
        tmp = [sb("tmp%d" % i, T, F32) for i in range(3)]
        sq = [sb("sq%d" % i, T, BF16) for i in range(4)]
        cbuf = [sb("cbuf%d" % i, 544, F32) for i in range(2)]
        pTT = sb("pTT", 2176, BF16)
        identb = sb("identb", 128, BF16)
        cbR = sb("cbR", DC, F32)
        vecs = sb("vecs", NV * DC, F32)
        ones = sb("ones", 128, BF16)
        wsT = sb("wsTb", 1024, BF16)
        Rb = sb("Rb", 1024, F32)
        sbias = sb("sbias", 1024, F32)
        haloA = sb("haloA", 2 * DC * 2, F32)
        haloC = sb("haloC", DC * 30, BF16)
        bst = sb("bst", 4 * 4 * 6, F32)
        mv = sb("mv", 8, F32)
        rs = sb("rs", 4, F32)
        ps = [st.enter_context(nc.psum_tensor("ps%d" % i, [128, 512], F32)) for i in range(8)]
        bank_ctr = [0]

        held = set()

        def bank():
            while True:
                b = bank_ctr[0] % 8
                bank_ctr[0] += 1
                if b not in held:
                    return b

        def vcol(vid, c):
            return vecs[:, vid * DC + c: vid * DC + c + 1]

        wq_list = []
        wstate = {'load': 0, 'use': 0}

        def w_prefetch():
            while wstate['load'] < len(wq_list) and wstate['load'] < wstate['use'] + NSLOT:
                i = wstate['load']
                ap, ncols = wq_list[i]
                s = i % NSLOT
                cx.dma('pool', 'w%d' % s,
                       lambda e, ap=ap, s=s, ncols=ncols: e.dma_start(out=wbuf[s][:, :ncols], in_=ap),
                       writes=[('w', s)])
                wstate['load'] += 1

        def w_get(ap):
            i = wstate['use']
            assert wq_list[i][0] is ap, "weight stream order mismatch at %d" % i
            w_prefetch()
            return i % NSLOT

        def w_done():
            wstate['use'] += 1
            w_prefetch()

        def plan_weights():
            def mixer_seq(i):
                kind, slot = i % 3, i // 3
                out = []
                if kind == 0:
                    for g in range(4):
                        out += [w_a_in[slot, g], w_a_in[slot, 4 + g], w_a_in[slot, 8 + g]]
                    out += [w_a_out[slot, n] for n in range(4)]
                elif kind == 1:
                    out += [w_b_in[slot, n] for n in range(8)]
                    out += [w_b_out[slot, n] for n in range(4)]
                else:
                    for g in range(4):
                        out += [w_c_in[slot, 4 + g], w_c_in[slot, g]]
                    out += [w_c_out[slot, n] for n in range(4)]
                return out
            seq = []
            for i in layers:
                for n in range(8):
                    seq += [(a, 4096) for a in w_kv[i, n]]
            per_tile = []
            for i in layers:
                for tl in mixer_seq(i):
                    per_tile += [(a, 4096) for a in tl]
                for n in range(4):
                    per_tile += [(a, 4096) for a in w_q[i, n]]
                for n in range(4):
                    per_tile += [(a, 4096) for a in w_o[i, n]]
                for n in range(22):
                    per_tile += [(a, 4096) for a in w_gu[i, n]]
                for n in range(8):
                    per_tile += [(a, 2816) for a in w_dn[i, n]]
            return seq, per_tile

        ap_cache = {}
        for (nm, t, shape) in [('a_in', w_a_in, (2, 12)), ('a_out', w_a_out, (2, 4)), ('b_in', w_b_in, (1, 8)),
                               ('b_out', w_b_out, (1, 4)), ('c_in', w_c_in, (1, 8)), ('c_out', w_c_out, (1, 4)),
                               ('q', w_q, (4, 4)), ('kv', w_kv, (4, 8)), ('o', w_o, (4, 4)), ('gu', w_gu, (4, 22))]:
            for a in range(shape[0]):
                for b in range(shape[1]):
                    ap_cache[(nm, a, b)] = [t[a, b, 0], t[a, b, 1]]
        for a in range(4):
            for b in range(8):
                ap_cache[('dn', a, b)] = [w_dn[a, b, c] for c in range(4)]

        class _W:
            def __init__(self, nm):
                self.nm = nm

            def __getitem__(self, idx):
                return ap_cache[(self.nm,) + tuple(idx)]
        w_a_in, w_a_out, w_b_in, w_b_out = _W('a_in'), _W('a_out'), _W('b_in'), _W('b_out')
        w_c_in, w_c_out, w_q, w_kv, w_o, w_gu, w_dn = _W('c_in'), _W('c_out'), _W('q'), _W('kv'), _W('o'), _W('gu'), _W('dn')
        pro_seq, tile_seq = plan_weights()
        wq_list.extend(pro_seq)
        for _ in range(NT):
            wq_list.extend(tile_seq)

        pending_pe = []
        marks = []

        def mark(lbl):
            marks.append((cx.ecnt['pe'], lbl))

        def flush_pe():
            items = list(pending_pe)
            pending_pe.clear()
            for f in items:
                f()

        def stat_mm(b, rhs_ap, rhs_res, first, last, ntok=T):
            cx.emit('pe', lambda e: e.matmul(ps[b][:, :ntok], ones[:, :], rhs_ap, start=first, stop=last),
                    reads=[rhs_res, 'consts'], writes=[('ps', b)])

        def recip(buf, bufn, ntok=T, scr=2):
            cx.emit('dve', lambda e: e.reciprocal(out=buf[:, :ntok], in_=buf[:, :ntok]), reads=[bufn], writes=[bufn])

        def finish_rstd(b, ntok=T, dst=None, dstn='rstd'):
            dst = rstd if dst is None else dst
            cx.emit('act', lambda e: e.activation(out=dst[:, :ntok], in_=ps[b][:, :ntok], func=AF.Sqrt, bias=EPS),
                    reads=[('ps', b)], writes=[dstn])
            recip(dst, dstn, ntok)

        def rms_stats(src, srcn, ntok=T):
            b = bank()
            for c in range(DC):
                q = sq[c % 4]
                cx.emit('act', lambda e, c=c, q=q: e.activation(out=q[:, :ntok], in_=src[:, c * ntok:(c + 1) * ntok],
                                                                func=AF.Square, scale=RSD),
                        reads=[(srcn, c)], writes=[('sq', c % 4)])
                stat_mm(b, q[:, :ntok], ('sq', c % 4), c == 0, c == DC - 1, ntok)
            finish_rstd(b, ntok)

        def rms_apply(src, srcn, vid, ntok=T):
            for c in range(DC):
                cx.emit('dve', lambda e, c=c: e.scalar_tensor_tensor(
                    out=h[:, c * ntok:(c + 1) * ntok], in0=src[:, c * ntok:(c + 1) * ntok], scalar=vcol(vid, c),
                    in1=rstd[:, :ntok], op0=ALU.mult, op1=ALU.mult),
                    reads=[(srcn, c), 'rstd', 'consts'], writes=[('h', c)])

        def rms_pre2(vid, need_r2=False, need_col=False):
            for c in range(DC):
                cx.emit('act', lambda e, c=c: e.activation(out=h[:, c * T:(c + 1) * T], in_=xs[:, c * T:(c + 1) * T],
                                                           func=AF.Identity, scale=vcol(vid, c)),
                        reads=[('xs', c), 'consts'], writes=[('h', c)])
            b = bank()
            held.add(b)
            for c in range(DC):
                cx.emit('act', lambda e, c=c: e.activation(out=big[:, c * T:(c + 1) * T], in_=xs[:, c * T:(c + 1) * T],
                                                           func=AF.Square, scale=RSD),
                        reads=[('xs', c)], writes=[('big', c)])
            for i in range(DC // 2):
                c0, c1 = 2 * i, 2 * i + 1
                cx.emit('dve', lambda e, c0=c0, c1=c1: e.tensor_tensor(
                    out=big[:, c0 * T:(c0 + 1) * T], in0=big[:, c0 * T:(c0 + 1) * T], in1=big[:, c1 * T:(c1 + 1) * T], op=ALU.add),
                    reads=[('big', c0), ('big', c1)], writes=[('big', c0)])
                pending_pe.append(lambda c0=c0, i=i: stat_mm(b, big[:, c0 * T:(c0 + 1) * T], ('big', c0), i == 0, i == DC // 2 - 1))

            def fin():
                finish_rstd(b, T, rin, 'rin')
                held.discard(b)
                if need_r2:
                    cx.emit('dve', lambda e: e.tensor_tensor(out=rden[:, :], in0=rin[:, :], in1=rin[:, :], op=ALU.mult),
                            reads=['rin'], writes=['rden'])
                if need_col:
                    bb = bank()
                    for tc in range(4):
                        cx.emit('pe', lambda e, tc=tc: e.matmul(ps[bb][:, tc:tc + 1], rin[:, tc * 128:(tc + 1) * 128], onesf[:, 0:1],
                                                                start=True, stop=True),
                                reads=['rin', 'consts'], writes=[('ps', bb)])
                    cx.emit('act', lambda e: e.activation(out=rcol[:, 0:4], in_=ps[bb][:, 0:4], func=AF.Copy),
                            reads=[('ps', bb)], writes=['rcol'])
            pending_pe.append(fin)

        def mul_rin(out_ap, out_res, b, src=None, srcn='rin'):
            src = rin if src is None else src
            cx.emit('dve', lambda e: e.tensor_tensor(out=out_ap, in0=ps[b][:, :], in1=src[:, :], op=ALU.mult),
                    reads=[('ps', b), srcn], writes=[out_res])

        def proj_fm(src, srcn, ntok, wtiles, kc, evac, nj=4):
            ncol = nj * 128
            for ng, kts in enumerate(wtiles):
                banks = [bank() for _ in range(nj)]
                held.update(banks)
                nkt = len(kts)
                for kt, ap in enumerate(kts):
                    s = w_get(ap)
                    for j in range(nj):
                        for k in range(kc):
                            kk = kt * kc + k
                            cx.emit('pe', lambda e, s=s, j=j, k=k, kk=kk, b=banks[j]: e.matmul(
                                ps[b][:, :ntok], wbuf[s][:, k * ncol + j * 128: k * ncol + (j + 1) * 128],
                                src[:, kk * ntok:(kk + 1) * ntok],
                                start=(kt == 0 and k == 0), stop=(kt == nkt - 1 and k == kc - 1)),
                                reads=[('w', s), (srcn, kk)], writes=[('ps', banks[j])])
                    w_done()
                flush_pe()
                for j in range(nj):
                    evac(ng * nj + j, banks[j])
                for f in late_evacs:
                    f()
                late_evacs.clear()
                for j in range(nj):
                    held.discard(banks[j])
            flush_pe()

        stat_bank = [0]

        def evac_y_begin():
            stat_bank[0] = bank()
            held.add(stat_bank[0])

        post_vid = [0]

        late_evacs = []

        def evac_y(c, b):
            q = sq[c % 4]
            cx.emit('act', lambda e: e.activation(out=q[:, :], in_=ps[b][:, :], func=AF.Square, scale=RSD),
                    reads=[('ps', b)], writes=[('sq', c % 4)])
            vid = post_vid[0]
            late_evacs.append(lambda: cx.emit('act', lambda e: e.activation(
                out=y[:, c * T:(c + 1) * T], in_=ps[b][:, :], func=AF.Identity, scale=vcol(vid, c)),
                reads=[('ps', b), 'consts'], writes=[('y', c)]))
            sbk = stat_bank[0]
            if c % 2 == 1:
                q0 = sq[(c - 1) % 4]
                cx.emit('dve', lambda e: e.tensor_tensor(out=q0[:, :], in0=q0[:, :], in1=q[:, :], op=ALU.add),
                        reads=[('sq', (c - 1) % 4), ('sq', c % 4)], writes=[('sq', (c - 1) % 4)])
                pending_pe.append(lambda: stat_mm(sbk, q0[:, :], ('sq', (c - 1) % 4), c == 1, c == DC - 1))

        def post_norm(vid):
            flush_pe()
            finish_rstd(stat_bank[0])
            held.discard(stat_bank[0])
            assert vid == post_vid[0]
            if NWARM:
                bw = bank()
                for _ in range(NWARM):
                    cx.emit('pe', lambda e: e.matmul(ps[bw][:, :], ones[:, :], wsT[:, 0:512], start=True, stop=True),
                            reads=['consts'], writes=[('ps', bw)])
            for c in range(DC):
                en = 'dve'
                cx.emit(en, lambda e, c=c: e.tensor_tensor(
                    out=y[:, c * T:(c + 1) * T], in0=y[:, c * T:(c + 1) * T], in1=rstd[:, :], op=ALU.mult),
                    reads=[('y', c), 'rstd'], writes=[('y', c)])
                cx.emit(en, lambda e, c=c: e.tensor_tensor(
                    out=xs[:, c * T:(c + 1) * T], in0=xs[:, c * T:(c + 1) * T], in1=y[:, c * T:(c + 1) * T], op=ALU.add),
                    reads=[('xs', c), ('y', c)], writes=[('xs', c)])

        def out_proj(wt, nm, i_slot, vid):
            mark('outproj_' + nm)
            post_vid[0] = vid
            evac_y_begin()
            proj_fm(big, 'big', T, [wt[i_slot, n] for n in range(4)], 8, evac_y)
            post_norm(vid)

        def conv_chunk(cb, cbn, K, vid_w, chunk, halo_ap, halo_res, first_bias_vid=None, out_ap=None, out_res=None):
            H = K - 1
            cx.emit('dve', lambda e: e.tensor_copy(out=cb[:, 0:H], in_=halo_ap), reads=[halo_res], writes=[cbn])
            cx.emit('dve', lambda e: e.tensor_copy(out=halo_ap, in_=cb[:, T:T + H]), reads=[cbn], writes=[halo_res])
            if first_bias_vid is None:
                cx.emit('dve', lambda e: e.tensor_scalar(out=out_ap, in0=cb[:, H:H + T], scalar1=vcol(vid_w(K - 1), chunk),
                                                         scalar2=None, op0=ALU.mult),
                        reads=[cbn, 'consts'], writes=[out_res])
            else:
                cx.emit('dve', lambda e: e.tensor_scalar(out=out_ap, in0=cb[:, H:H + T], scalar1=vcol(vid_w(K - 1), chunk),
                                                         scalar2=vcol(first_bias_vid, chunk), op0=ALU.mult, op1=ALU.add),
                        reads=[cbn, 'consts'], writes=[out_res])
            for k in range(K - 2, -1, -1):
                cx.emit('dve', lambda e, k=k: e.scalar_tensor_tensor(
                    out=out_ap, in0=cb[:, k:k + T], scalar=vcol(vid_w(k), chunk), in1=out_ap,
                    op0=ALU.mult, op1=ALU.add),
                    reads=[cbn, out_res, 'consts'], writes=[out_res])

        def mixer_a(i):
            mark('A_in')
            slot = i // 3
            rms_pre2(V_MIX(i, 0), need_r2=True)
            for g in range(4):
                def ev_b(c, b):
                    j = c % 4
                    mul_rin(gt[:, j * T:(j + 1) * T], ('gt', j), b)

                def ev_c(c, b):
                    j = c % 4
                    mul_rin(y[:, j * T:(j + 1) * T], ('y', j), b, rden, 'rden')

                def ev_z(c, b, g=g):
                    j = c % 4
                    chunk = g * 4 + j
                    cb = cbuf[chunk % 2]
                    cbn = ('cbuf', chunk % 2)
                    cx.emit('dve', lambda e: e.tensor_tensor(out=cb[:, 2:2 + T], in0=y[:, j * T:(j + 1) * T],
                                                             in1=ps[b][:, :], op=ALU.mult),
                            reads=[('y', j), ('ps', b)], writes=[cbn])
                    hoff = (slot * DC + chunk) * 2
                    t_ = tmp[chunk % 3]
                    conv_chunk(cb, cbn, 3, lambda k: V_ACONV(slot, k), chunk, haloA[:, hoff:hoff + 2], ('haloA', slot, chunk),
                               out_ap=t_[:, :], out_res=('tmp', chunk % 3))
                    cx.emit('dve', lambda e: e.tensor_tensor(out=big[:, chunk * T:(chunk + 1) * T], in0=t_[:, :],
                                                             in1=gt[:, j * T:(j + 1) * T], op=ALU.mult),
                            reads=[('tmp', chunk % 3), ('gt', j)], writes=[('big', chunk)])
                proj_fm(h, 'h', T, [w_a_in[slot, g]], 8, ev_b)
                proj_fm(h, 'h', T, [w_a_in[slot, 4 + g]], 8, ev_c)
                proj_fm(h, 'h', T, [w_a_in[slot, 8 + g]], 8, ev_z)
            out_proj(w_a_out, 'a_out', slot, V_MIX(i, 1))

        def mixer_c(i):
            mark('C_in')
            slot = i // 3
            rms_pre2(V_MIX(i, 0))
            b1, b2 = bank(), bank()
            held.add(b1)
            held.add(b2)
            DG0 = 16

            def dg_load(c):
                s3 = c % 3
                base = (DG0 + 8 * s3) * T
                cx.dma('sp', 'dg%d' % s3, lambda e: e.dma_start(out=big[:, base:base + 3968], in_=dgc[c]),
                       reads=[('dgc', c)], writes=[('big', DG0 + 8 * s3 + n) for n in range(8)])
            for c in range(3):
                dg_load(c)
            prev_stat = [None]

            def run_prev_stat():
                if prev_stat[0] is not None:
                    prev_stat[0]()
                    prev_stat[0] = None

            for g in range(4):
                def ev_g(c, b):
                    j = c % 4
                    mul_rin(gt[:, j * T:(j + 1) * T], ('gt', j), b)
                    cx.emit('act', lambda e: e.activation(out=gt[:, j * T:(j + 1) * T], in_=gt[:, j * T:(j + 1) * T], func=AF.Sigmoid),
                            reads=[('gt', j)], writes=[('gt', j)])

                def ev_a(c, b, g=g):
                    j = c % 4
                    chunk = g * 4 + j
                    off = j * 544
                    cbn = ('cbb', j)
                    t_ = tmp[j % 3]
                    mul_rin(t_[:, :], ('tmp', j % 3), b)
                    cx.emit('dve', lambda e: e.tensor_tensor(out=pTT[:, off + 30: off + 30 + T], in0=gt[:, j * T:(j + 1) * T],
                                                             in1=t_[:, :], op=ALU.mult),
                            reads=[('gt', j), ('tmp', j % 3)], writes=[cbn])
                    hres = ('haloC', chunk)
                    cx.emit('dve', lambda e: e.tensor_copy(out=pTT[:, off: off + 30], in_=haloC[:, chunk * 30:(chunk + 1) * 30]),
                            reads=[hres], writes=[cbn])
                    cx.emit('dve', lambda e: e.tensor_copy(out=haloC[:, chunk * 30:(chunk + 1) * 30], in_=pTT[:, off + T: off + T + 30]),
                            reads=[cbn], writes=[hres])

                    def conv():
                        s3 = chunk % 3
                        base = (DG0 + 8 * s3) * T
                        dgres = [('big', DG0 + 8 * s3 + n) for n in range(8)]
                        bc = bank()
                        for k in range(31):
                            cx.emit('pe', lambda e, k=k: e.matmul(ps[bc][:, :], big[:, base + k * 128: base + (k + 1) * 128],
                                                                  pTT[:, off + k: off + k + T], start=(k == 0), stop=(k == 30)),
                                    reads=dgres + [cbn], writes=[('ps', bc)])
                        if chunk + 3 < DC:
                            dg_load(chunk + 3)
                        run_prev_stat()
                        q1, q2 = sq[(2 * chunk) % 4], sq[(2 * chunk + 1) % 4]
                        cx.emit('act', lambda e: e.activation(out=y[:, chunk * T:(chunk + 1) * T], in_=ps[bc][:, :], func=AF.Identity,
                                                              bias=vcol(V_CB, chunk)),
                                reads=[('ps', bc), 'consts'], writes=[('y', chunk)])
                        cx.emit('act', lambda e: e.activation(out=q1[:, :], in_=ps[bc][:, :], func=AF.Identity,
                                                              bias=vcol(V_CB, chunk)),
                                reads=[('ps', bc), 'consts'], writes=[('sq', (2 * chunk) % 4)])
                        cx.emit('act', lambda e: e.activation(out=q2[:, :], in_=ps[bc][:, :], func=AF.Square, scale=RSD,
                                                              bias=cbR[:, chunk:chunk + 1]),
                                reads=[('ps', bc), 'consts'], writes=[('sq', (2 * chunk + 1) % 4)])

                        def stat():
                            stat_mm(b1, q1[:, :], ('sq', (2 * chunk) % 4), chunk == 0, chunk == DC - 1)
                            stat_mm(b2, q2[:, :], ('sq', (2 * chunk + 1) % 4), chunk == 0, chunk == DC - 1)
                        prev_stat[0] = stat
                    pending_pe.append(conv)
                proj_fm(h, 'h', T, [w_c_in[slot, 4 + g]], 8, ev_g)
                proj_fm(h, 'h', T, [w_c_in[slot, g]], 8, ev_a)
            flush_pe()
            run_prev_stat()
            cx.emit('act', lambda e: e.activation(out=meanb[:, :], in_=ps[b1][:, :], func=AF.Copy, scale=1.0 / D),
                    reads=[('ps', b1)], writes=['rin'])
            cx.emit('dve', lambda e: e.tensor_tensor(out=rden[:, :], in0=meanb[:, :], in1=meanb[:, :], op=ALU.mult),
                    reads=['rin'], writes=['rden'])
            cx.emit('dve', lambda e: e.tensor_tensor(out=rstd[:, :], in0=ps[b2][:, :], in1=rden[:, :], op=ALU.subtract),
                    reads=[('ps', b2), 'rden'], writes=['rstd'])
            held.discard(b1)
            held.discard(b2)
            cx.emit('act', lambda e: e.activation(out=rstd[:, :], in_=rstd[:, :], func=AF.Sqrt, bias=EPS),
                    reads=['rstd'], writes=['rstd'])
            cx.emit('dve', lambda e: e.reciprocal(out=rstd[:, :], in_=rstd[:, :]), reads=['rstd'], writes=['rstd'])
            for c in range(DC):
                t_ = tmp[c % 3]
                cx.emit('dve', lambda e, c=c, t_=t_: e.tensor_tensor(out=t_[:, :], in0=y[:, c * T:(c + 1) * T],
                                                                     in1=meanb[:, :], op=ALU.subtract),
                        reads=[('y', c), 'rin'], writes=[('tmp', c % 3)])
                cx.emit('dve', lambda e, t_=t_: e.tensor_tensor(out=t_[:, :], in0=t_[:, :], in1=rstd[:, :], op=ALU.mult),
                        reads=[('tmp', c % 3), 'rstd'], writes=[('tmp', c % 3)])
                cx.emit('act', lambda e, c=c, t_=t_: e.activation(out=big[:, c * T:(c + 1) * T], in_=t_[:, :], func=AF.Silu,
                                                                  scale=vcol(V_CLG, c), bias=vcol(V_CLB, c)),
                        reads=[('tmp', c % 3), 'consts'], writes=[('big', c)])
            out_proj(w_c_out, 'c_out', slot, V_MIX(i, 1))

        VH = DC * T

        def mixer_b(i):
            mark('B_in')
            slot = i // 3
            rms_pre2(V_MIX(i, 0), need_col=True)

            def ev_u(c, b):
                t_ = tmp[c % 2]
                mul_rin(t_[:, :], ('tmp', c % 2), b)
                cx.emit('act', lambda e: e.activation(out=big[:, c * T:(c + 1) * T], in_=t_[:, :], func=AF.Gelu_apprx_tanh),
                        reads=[('tmp', c % 2)], writes=[('big', c)])
            proj_fm(h, 'h', T, [w_b_in[slot, n] for n in range(4)], 8, ev_u)
            for n in range(4):
                banks = [bank() for _ in range(4)]
                for kt, ap in enumerate(w_b_in[slot, 4 + n]):
                    s = w_get(ap)
                    for tc in range(4):
                        for k in range(8):
                            kk = kt * 8 + k
                            cx.emit('pe', lambda e, s=s, tc=tc, k=k, kk=kk, b=banks[tc]: e.matmul(
                                ps[b][:, :], h[:, kk * T + tc * 128: kk * T + (tc + 1) * 128], wbuf[s][:, k * 512:(k + 1) * 512],
                                start=(kk == 0), stop=(kk == DC - 1)),
                                reads=[('w', s), ('h', kk)], writes=[('ps', banks[tc])])
                    w_done()
                for tc in range(4):
                    cx.emit('act', lambda e, tc=tc, n=n, b=banks[tc]: e.activation(
                        out=y[:, tc * 2048 + n * 512: tc * 2048 + (n + 1) * 512], in_=ps[b][:, :], func=AF.Gelu_apprx_tanh,
                        scale=rcol[:, tc:tc + 1]),
                        reads=[('ps', banks[tc]), 'rcol'], writes=[('y', tc * 4 + n)])
            for tc in range(4):
                for n in range(4):
                    cx.emit('dve', lambda e, tc=tc, n=n: e.bn_stats(out=bst[:, (tc * 4 + n) * 6:(tc * 4 + n + 1) * 6],
                                                                    in_=y[:, tc * 2048 + n * 512: tc * 2048 + (n + 1) * 512]),
                            reads=[('y', tc * 4 + n)], writes=[('bst', tc * 4 + n)])
                cx.emit('dve', lambda e, tc=tc: e.bn_aggr(out=mv[:, tc * 2:tc * 2 + 2], in_=bst[:, tc * 24:(tc + 1) * 24]),
                        reads=[('bst', tc * 4 + n) for n in range(4)], writes=[('mv', tc)])
                cx.emit('act', lambda e, tc=tc: e.activation(out=rs[:, tc:tc + 1], in_=mv[:, tc * 2 + 1:tc * 2 + 2], func=AF.Sqrt, bias=EPS),
                        reads=[('mv', tc)], writes=[('rs', tc)])
                cx.emit('dve', lambda e, tc=tc: e.reciprocal(out=rs[:, tc:tc + 1], in_=rs[:, tc:tc + 1]),
                        reads=[('rs', tc)], writes=[('rs', tc)])
                cx.emit('dve', lambda e, tc=tc: e.tensor_scalar(
                    out=big[:, VH + tc * 2048: VH + (tc + 1) * 2048], in0=y[:, tc * 2048:(tc + 1) * 2048],
                    scalar1=mv[:, tc * 2:tc * 2 + 1], scalar2=rs[:, tc:tc + 1], op0=ALU.subtract, op1=ALU.mult),
                    reads=[('y', tc * 4 + n) for n in range(4)] + [('mv', tc), ('rs', tc)],
                    writes=[('big', DC + tc * 4 + n) for n in range(4)])
            for c in range(DC):
                g = c // 2
                b = bank()
                for n in range(4):
                    cx.emit('pe', lambda e, n=n, c=c, g=g, b=b: e.matmul(
                        ps[b][:, n * 128:(n + 1) * 128], big[:, VH + n * 2048 + c * 128: VH + n * 2048 + (c + 1) * 128],
                        wsT[:, g * 128:(g + 1) * 128], start=True, stop=True),
                        reads=[('big', DC + n * 4 + c // 4), 'consts'], writes=[('ps', b)])
                t0, t1 = tmp[0], tmp[1]
                cx.emit('dve', lambda e, c=c, g=g: e.scalar_tensor_tensor(
                    out=t0[:, 0:128], in0=Rb[:, g * 128:(g + 1) * 128], scalar=vcol(V_BVB, c),
                    in1=sbias[:, g * 128:(g + 1) * 128], op0=ALU.mult, op1=ALU.add),
                    reads=['consts'], writes=[('tmp', 0)])
                for n in range(4):
                    cx.emit('dve', lambda e, c=c, n=n, b=b: e.scalar_tensor_tensor(
                        out=t1[:, n * 128:(n + 1) * 128], in0=ps[b][:, n * 128:(n + 1) * 128], scalar=vcol(V_BVG, c),
                        in1=t0[:, 0:128], op0=ALU.mult, op1=ALU.add),
                        reads=[('ps', b), ('tmp', 0), 'consts'], writes=[('tmp', 1)])
                cx.emit('dve', lambda e, c=c: e.tensor_tensor(out=big[:, c * T:(c + 1) * T], in0=big[:, c * T:(c + 1) * T],
                                                              in1=t1[:, :], op=ALU.mult),
                        reads=[('big', c), ('tmp', 1)], writes=[('big', c)])
            out_proj(w_b_out, 'b_out', slot, V_MIX(i, 1))

        KT0 = DC * T
        V0 = DC * T + 4096
        SCALE = 512.0 ** -0.5

        def xattn(i):
            mark('xattn_q')
            cx.dma('sp', 'kvld', lambda e: e.dma_start(out=big[:, KT0:KT0 + 8192], in_=kvc[i]),
                   reads=[('kvc', i)], writes=[('big', DC + n) for n in range(16)])
            rms_pre2(V_XA(i, 0))

            def ev_q(c, b):
                mul_rin(big[:, c * T:(c + 1) * T], ('big', c), b)
            proj_fm(h, 'h', T, [w_q[i, n] for n in range(4)], 8, ev_q)
            kvres = [('big', DC + n) for n in range(16)]
            mark('xattn_core')
            for hd in range(4):
                po = (hd % 2) * 1024
                pn = ('pT', hd % 2)
                for mc in range(2):
                    b = bank()
                    for j in range(4):
                        ch = hd * 4 + j
                        cx.emit('pe', lambda e, ch=ch, mc=mc, j=j, b=b: e.matmul(
                            ps[b][:, :], big[:, KT0 + ch * 256 + mc * 128: KT0 + ch * 256 + (mc + 1) * 128],
                            big[:, ch * T:(ch + 1) * T], start=(j == 0), stop=(j == 3)),
                            reads=kvres + [('big', ch)], writes=[('ps', b)])
                    cx.emit('act', lambda e, mc=mc, b=b, po=po: e.activation(out=pTT[:, po + mc * T: po + (mc + 1) * T], in_=ps[b][:, :],
                                                                           func=AF.Exp, scale=SCALE),
                            reads=[('ps', b)], writes=[(pn, mc)])
                bd = bank()
                for mc in range(2):
                    cx.emit('pe', lambda e, mc=mc, po=po: e.matmul(ps[bd][:, :], ones[:, :], pTT[:, po + mc * T: po + (mc + 1) * T],
                                                                   start=(mc == 0), stop=(mc == 1)),
                            reads=[(pn, mc), 'consts'], writes=[('ps', bd)])
                cx.emit('dve', lambda e: e.reciprocal(out=rden[:, :], in_=ps[bd][:, :]), reads=[('ps', bd)], writes=['rden'])
                for j in range(4):
                    ch = hd * 4 + j
                    b = bank()
                    for mc in range(2):
                        cx.emit('pe', lambda e, ch=ch, mc=mc, b=b, po=po: e.matmul(
                            ps[b][:, :], big[:, V0 + mc * 2048 + ch * 128: V0 + mc * 2048 + (ch + 1) * 128],
                            pTT[:, po + mc * T: po + (mc + 1) * T], start=(mc == 0), stop=(mc == 1)),
                            reads=kvres + [(pn, mc)], writes=[('ps', b)])
                    cx.emit('dve', lambda e, ch=ch, b=b: e.tensor_tensor(out=big[:, ch * T:(ch + 1) * T], in0=ps[b][:, :],
                                                                        in1=rden[:, :], op=ALU.mult),
                            reads=[('ps', b), 'rden'], writes=[('big', ch)])
            out_proj(w_o, 'o', i, V_XA(i, 1))

        def ffn(i):
            mark('ffn_gu')
            rms_pre2(V_FFN(i, 0))
            for grp in range(22):
                def ev_gu(c, b, grp=grp):
                    j = c % 4
                    if j < 2:
                        t_ = tmp[j]
                        mul_rin(t_[:, :], ('tmp', j), b)
                        cx.emit('act', lambda e: e.activation(out=t_[:, :], in_=t_[:, :], func=AF.Silu),
                                reads=[('tmp', j)], writes=[('tmp', j)])
                    else:
                        t_ = tmp[j - 2]
                        hc = grp * 2 + (j - 2)
                        mul_rin(gt[:, (j - 2) * T:(j - 1) * T], ('gt', j - 2), b)
                        cx.emit('dve', lambda e: e.tensor_tensor(out=big[:, hc * T:(hc + 1) * T], in0=t_[:, :],
                                                                 in1=gt[:, (j - 2) * T:(j - 1) * T], op=ALU.mult),
                                reads=[('tmp', j - 2), ('gt', j - 2)], writes=[('big', hc)])
                proj_fm(h, 'h', T, [w_gu[i, grp]], 8, ev_gu)
            mark('ffn_down')
            post_vid[0] = V_FFN(i, 1)
            evac_y_begin()
            proj_fm(big, 'big', T, [w_dn[i, n] for n in range(8)], 11, evac_y, nj=2)
            post_norm(V_FFN(i, 1))

        def load_x(t):
            for c in range(DC):
                cx.dma('sp', 'xld%d' % c, lambda e, c=c: e.dma_start(out=xs[:, c * T:(c + 1) * T], in_=xT[t, :, c * T:(c + 1) * T]),
                       writes=[('xs', c)])

        load_x(0)
        cx.dma('sp', 'cst', lambda e: e.dma_start(out=vecs[:, :], in_=vecs_d), writes=['consts'])
        cx.dma('sp', 'cst', lambda e: e.dma_start(out=sbias[:, :], in_=sbias_d), writes=['consts'])
        cx.dma('sp', 'cst', lambda e: e.dma_start(out=y[:, 0:1024], in_=wsT_d), writes=['consts', ('y', 0), ('y', 1)])
        cx.emit('dve', lambda e: e.memset(ones[:, :], 1.0), writes=['consts'])
        cx.emit('dve', lambda e: e.memset(onesf[:, :], 1.0 / 128.0), writes=['consts'])
        cx.emit('dve', lambda e: e.memset(haloA[:, :], 0.0), writes=[('haloA', s_, c) for s_ in range(2) for c in range(DC)])
        cx.emit('dve', lambda e: e.memset(haloC[:, :], 0.0), writes=[('haloC', c) for c in range(DC)])
        if 1 in layers:
            cx.emit('pool', lambda e: e.affine_select(out=wsT[:, :], in_=y[:, 0:1024], pattern=[[0, 8], [1, 128]],
                                                       compare_op=ALU.is_ge, fill=0.0, base=0, channel_multiplier=-1),
                    reads=['consts', ('y', 0), ('y', 1)], writes=['consts'])
            for hf in range(2):
                b = bank()
                for g4 in range(4):
                    g = hf * 4 + g4
                    cx.emit('pe', lambda e, g=g, g4=g4, b=b: e.matmul(ps[b][:, g4 * 128:(g4 + 1) * 128], ones[:, :],
                                                                      wsT[:, g * 128:(g + 1) * 128], start=True, stop=True),
                            reads=['consts'], writes=[('ps', b)])
                cx.emit('act', lambda e, hf=hf, b=b: e.activation(out=Rb[:, hf * 512:(hf + 1) * 512], in_=ps[b][:, :], func=AF.Copy),
                        reads=[('ps', b)], writes=['consts'])
        def dg_setup():
            cx.emit('dve', lambda e: e.memset(identb[:, :], 1.0), writes=['identb'])
            cx.emit('pool', lambda e: e.affine_select(out=identb[:, :], in_=identb[:, :], pattern=[[1, 128]],
                                                       compare_op=ALU.is_equal, fill=0.0, base=0, channel_multiplier=-1),
                    reads=['identb'], writes=['identb'])
            cx.emit('dve', lambda e: e.tensor_scalar(out=cbR[:, 0:DC], in0=vecs[:, V_CB * DC:(V_CB + 1) * DC], scalar1=RSD,
                                                     scalar2=None, op0=ALU.mult),
                    reads=['consts'], writes=['consts'])

        def dg_build(c):
            sl = c % 2
            base = (28 + sl * 8) * T
            for k in range(31):
                cx.emit('dve', lambda e, k=k: e.tensor_scalar(
                    out=big[:, base + k * 128: base + (k + 1) * 128], in0=identb[:, :], scalar1=vcol(V_CCONV(k), c),
                    scalar2=None, op0=ALU.mult),
                    reads=['identb', 'consts'], writes=[('big', 28 + sl * 8 + k // 4)])
            cx.dma('sp', 'dgst', lambda e: e.dma_start(out=dgc[c], in_=big[:, base:base + 3968]),
                   reads=[('big', 28 + sl * 8 + n) for n in range(8)], writes=[('dgc', c)])

        dg_todo = list(range(DC)) if 2 in layers else []
        if dg_todo:
            dg_setup()
        cx.dma('sp', 'mld', lambda e: e.dma_start(out=y[:, 0:DC * NMEM], in_=memT),
               writes=[('y', c) for c in range(DC)])
        rms_stats(y, 'y', ntok=NMEM)
        for i in layers:
            rms_apply(y, 'y', V_XA(i, 2), ntok=NMEM)
            nshare = (len(dg_todo) + (len(layers) - layers.index(i)) - 1) // (len(layers) - layers.index(i))
            for _ in range(nshare):
                dg_build(dg_todo.pop(0))

            def ev_k(c, b):
                cx.emit('act', lambda e: e.activation(out=big[:, c * 256:(c + 1) * 256], in_=ps[b][:, :256], func=AF.Copy),
                        reads=[('ps', b)], writes=[('big', c)])
            proj_fm(h, 'h', NMEM, [w_kv[i, n] for n in range(4)], 8, ev_k)
            for n in range(4):
                banks = [bank() for _ in range(2)]
                for kt, ap in enumerate(w_kv[i, 4 + n]):
                    s = w_get(ap)
                    for tc in range(2):
                        for k in range(8):
                            kk = kt * 8 + k
                            cx.emit('pe', lambda e, s=s, tc=tc, k=k, kk=kk, b=banks[tc]: e.matmul(
                                ps[b][:, :], h[:, kk * NMEM + tc * 128: kk * NMEM + (tc + 1) * 128], wbuf[s][:, k * 512:(k + 1) * 512],
                                start=(kk == 0), stop=(kk == DC - 1)),
                                reads=[('w', s), ('h', kk)], writes=[('ps', banks[tc])])
                    w_done()
                for tc in range(2):
                    cx.emit('act', lambda e, tc=tc, n=n, b=banks[tc]: e.activation(
                        out=big[:, 4096 + tc * 2048 + n * 512: 4096 + tc * 2048 + (n + 1) * 512], in_=ps[b][:, :], func=AF.Copy),
                        reads=[('ps', banks[tc])], writes=[('big', 16 + tc * 4 + n)])
            cx.dma('sp', 'kvst', lambda e, i=i: e.dma_start(out=kvc[i], in_=big[:, 0:8192]),
                   reads=[('big', c) for c in range(24)], writes=[('kvc', i)])

        assert not dg_todo
        if 2 in layers:
            for c in range(DC):
                cx.last_w[('dgc', c)] = cx.last_w[('dgc', DC - 1)]
        for t in range(NT):
            if t > 0:
                load_x(t)
            for i in layers:
                [mixer_a, mixer_b, mixer_c][i % 3](i)
                if stop == 'mixer':
                    break
                xattn(i)
                if stop == 'xattn':
                    break
                ffn(i)
            for c in range(DC):
                cx.dma('sp', 'ost%d' % c, lambda e, t=t, c=c: e.dma_start(out=outT[t, :, c * T:(c + 1) * T], in_=xs[:, c * T:(c + 1) * T]),
                       reads=[('xs', c)], writes=[('out', t, c)])
        if dbg:
            allres = list(cx.last_w.keys())
            for (nm, buf, cols, dt) in [('h', h, DC * T, BF16), ('y', y, DC * T, F32), ('big', big, FC * T, BF16),
                                        ('rstd', rstd, T, F32), ('vecs', vecs, NV * DC, F32), ('gt', gt, 4 * T, F32)]:
                dd = nc.dram_tensor("dbg_" + nm, [128, cols], dt, kind="ExternalOutput").ap()
                cx.dma('sp', 'ost', lambda e, dd=dd, buf=buf: e.dma_start(out=dd, in_=buf[:, :]), reads=allres, writes=[('out', 'dbg' + nm)])
            cx.wait_all('sp', [('out', 'dbg' + nm) for nm in ['h', 'y', 'big', 'rstd', 'vecs', 'gt']])
        cx.wait_all('sp', [('out', t, c) for t in range(NT) for c in range(DC)])
        assert stop or wstate['use'] == len(wq_list), (wstate, len(wq_list))
        print("sbuf bytes remaining", nc.sbuf_bytes_remaining)
        print("instructions", cx.n_inst, "waits", cx.n_wait, "pe", cx.ecnt['pe'])
    return nc


def tile_w(w, kc, ncol=512):
    K, N = w.shape
    nkt = K // (128 * kc)
    ng = N // ncol
    a = w.reshape(nkt, kc, 128, ng, ncol).transpose(3, 0, 2, 1, 4)
    return np.ascontiguousarray(a).reshape(ng, nkt, 128, kc * ncol)


def prep_shared(inp):
    f = lambda a: np.asarray(a, dtype=np.float32)
    sh = {}
    rows = [f(inp['mix_norm']).reshape(8, D), f(inp['xa_norm']).reshape(12, D), f(inp['ffn_norm']).reshape(8, D),
            f(inp['a_conv_w']).reshape(6, D), f(inp['c_conv_w']).reshape(31, D), f(inp['c_conv_b']).reshape(1, D),
            f(inp['c_ln_g']).reshape(1, D), f(inp['c_ln_b']).reshape(1, D), f(inp['b_v_g']).reshape(1, D),
            f(inp['b_v_b']).reshape(1, D)]
    allv = np.concatenate(rows, axis=0)
    assert allv.shape[0] == NV
    sh['vecs'] = np.ascontiguousarray(allv.reshape(NV, DC, 128).transpose(2, 0, 1)).reshape(128, NV * DC)
    ws = f(inp['b_w_s'])[0]
    sh['wsT'] = np.ascontiguousarray(ws.transpose(2, 0, 1)).reshape(128, 8 * 128)
    sb_ = f(inp['b_s_bias'])[0].reshape(1, 8 * 128)
    sh['sbias'] = np.ascontiguousarray(np.broadcast_to(sb_, (128, 1024)))

    def tw(name, kc=8):
        w = f(inp[name])
        return np.stack([tile_w(w[l], kc) for l in range(w.shape[0])])
    for name in ['a_w_in', 'a_w_out', 'b_w_in', 'b_w_out', 'c_w_in', 'c_w_out', 'xa_wq', 'xa_wkv', 'xa_wo']:
        sh[name] = tw(name)
    gu = f(inp['ffn_w_gu'])
    gl = []
    for l in range(gu.shape[0]):
        gate = gu[l][:, :DFF].reshape(D, 22, 256)
        up = gu[l][:, DFF:].reshape(D, 22, 256)
        wi = np.concatenate([gate, up], axis=2).reshape(D, 2 * DFF)
        gl.append(tile_w(wi, 8))
    sh['ffn_w_gu'] = np.stack(gl)
    dn = f(inp['ffn_w_down'])
    sh['ffn_w_down'] = np.stack([tile_w(dn[l], 11, 256) for l in range(dn.shape[0])])
    return sh


def prep_core(x_b, mem_b, NT):
    xt = x_b[:NT * T].reshape(NT, T, DC, 128).transpose(0, 3, 2, 1)
    xt = np.ascontiguousarray(xt).reshape(NT, 128, DC * T)
    mt = np.ascontiguousarray(mem_b.reshape(NMEM, DC, 128).transpose(2, 1, 0)).reshape(128, DC * NMEM)
    return {'xT': xt, 'memT': mt}


def unprep_out(o, NT):
    return np.ascontiguousarray(o.reshape(NT, 128, DC, T).transpose(0, 3, 2, 1)).reshape(NT * T, D)


def kernel(**inputs):
    NT = SEQ // T
    x = np.asarray(inputs['x'], dtype=np.float32)
    mem = np.asarray(inputs['mem'], dtype=np.float32)
    sh = prep_shared(inputs)
    nc = build(NT, [0, 1, 2, 3])
    in_maps = []
    for b in range(8):
        m = dict(sh)
        m.update(prep_core(x[b], mem[b], NT))
        in_maps.append(m)
    res = run_bass_kernel_spmd(nc, in_maps, core_ids=list(range(8)))
    out = np.stack([unprep_out(np.asarray(res.results[b]['outT']), NT) for b in range(8)])
    return out.astype(np.float32)
```

```python
import numpy as np
from contextlib import ExitStack
import concourse.bass as bass
import concourse.mybir as mybir
from concourse.bass_utils import run_bass_kernel_spmd

F32 = mybir.dt.float32
BF16 = mybir.dt.bfloat16
AF = mybir.ActivationFunctionType
ALU = mybir.AluOpType

D = 2048
DC = 16
T = 512
SEQ = 4096
NMEM = 256
DFF = 5632
FC = 44
EPS = 1e-6
NV = 70
NSLOT = 4
NWARM = 0
WCOLS = 4096
SAME_SYNC = True
RSD = float(D) ** -0.5


class Ctx:
    def __init__(self, nc, stack):
        self.nc = nc
        self.eng = {'pe': nc.tensor, 'act': nc.scalar, 'dve': nc.vector,
                    'pool': nc.gpsimd, 'sp': nc.sync}
        self.sems = {}
        self.ecnt = {e: 0 for e in self.eng}
        self.dcnt = {}
        self.waited = {e: {} for e in self.eng}
        self.last_w = {}
        self.readers = {}
        self.stack = stack
        for e in self.eng:
            self.sems['e_' + e] = stack.enter_context(nc.semaphore('s_e_' + e))
        self.n_inst = 0
        self.n_wait = 0

    def dma_sem(self, name):
        self.sems[name] = self.stack.enter_context(self.nc.semaphore('s_' + name))
        self.dcnt[name] = 0

    def _waits(self, e, reads, writes):
        need = {}
        for r in reads:
            ev = self.last_w.get(r)
            if ev is not None and need.get(ev[0], 0) < ev[1]:
                need[ev[0]] = ev[1]
        for w in writes:
            ev = self.last_w.get(w)
            if ev is not None and need.get(ev[0], 0) < ev[1]:
                need[ev[0]] = ev[1]
            for k, v in self.readers.get(w, {}).items():
                if need.get(k, 0) < v:
                    need[k] = v
        own = 'e_' + e
        for k, v in need.items():
            if k == own and (e == 'pe' or e == 'sp' or not SAME_SYNC):
                continue
            if self.waited[e].get(k, 0) >= v:
                continue
            self.eng[e].wait_ge(self.sems[k], v)
            self.waited[e][k] = v
            self.n_wait += 1

    def _record(self, ev, reads, writes):
        k, v = ev
        for r in reads:
            d = self.readers.setdefault(r, {})
            if d.get(k, 0) < v:
                d[k] = v
        for w in writes:
            self.last_w[w] = ev
            self.readers[w] = {}

    def emit(self, e, fn, reads=(), writes=()):
        self._waits(e, reads, writes)
        inst = fn(self.eng[e])
        self.ecnt[e] += 1
        inst.then_inc(self.sems['e_' + e], 1)
        self._record(('e_' + e, self.ecnt[e]), reads, writes)
        self.n_inst += 1
        return inst

    def dma(self, q, semname, fn, reads=(), writes=()):
        self._waits(q, reads, writes)
        inst = fn(self.eng[q])
        self.dcnt[semname] += 16
        inst.then_inc(self.sems[semname], 16)
        self._record((semname, self.dcnt[semname]), reads, writes)
        self.n_inst += 1
        return inst

    def wait_all(self, e, resources):
        self._waits(e, list(resources), [])


def V_MIX(i, j): return i * 2 + j
def V_XA(i, j): return 8 + i * 3 + j
def V_FFN(i, j): return 20 + i * 2 + j
def V_ACONV(s, k): return 28 + s * 3 + k
def V_CCONV(k): return 34 + k
V_CB, V_CLG, V_CLB, V_BVG, V_BVB = 65, 66, 67, 68, 69


def build(NT, layers, stop=None, dbg=False):
    nc = bass.Bass("TRN2", target_bir_lowering=False)

    def din(name, shape, dt=F32):
        return nc.dram_tensor(name, shape, dt, kind="ExternalInput").ap()

    xT = din("xT", [NT, 128, DC * T])
    memT = din("memT", [128, DC * NMEM])
    vecs_d = din("vecs", [128, NV * DC])
    wsT_d = din("wsT", [128, 8 * 128])
    sbias_d = din("sbias", [128, 8 * 128])
    w_a_in = din("a_w_in", [2, 12, 2, 128, 4096])
    w_a_out = din("a_w_out", [2, 4, 2, 128, 4096])
    w_b_in = din("b_w_in", [1, 8, 2, 128, 4096])
    w_b_out = din("b_w_out", [1, 4, 2, 128, 4096])
    w_c_in = din("c_w_in", [1, 8, 2, 128, 4096])
    w_c_out = din("c_w_out", [1, 4, 2, 128, 4096])
    w_q = din("xa_wq", [4, 4, 2, 128, 4096])
    w_kv = din("xa_wkv", [4, 8, 2, 128, 4096])
    w_o = din("xa_wo", [4, 4, 2, 128, 4096])
    w_gu = din("ffn_w_gu", [4, 22, 2, 128, 4096])
    w_dn = din("ffn_w_down", [4, 8, 4, 128, 11 * 256])
    outT = nc.dram_tensor("outT", [NT, 128, DC * T], F32, kind="ExternalOutput").ap()
    kvc = nc.dram_tensor("kvc", [4, 128, 8192], BF16).ap()
    dgc = nc.dram_tensor("dgc", [DC, 128, 3968], BF16).ap()

    with ExitStack() as st:
        cx = Ctx(nc, st)
        for n in ['cst', 'kvst', 'kvld', 'mld', 'ost', 'dgst', 'dg0', 'dg1', 'dg2'] + ['w%d' % s for s in range(NSLOT)] + ['xld%d' % c for c in range(DC)] + ['ost%d' % c for c in range(DC)]:
            cx.dma_sem(n)

        def sb(name, cols, dt):
            return st.enter_context(nc.sbuf_tensor("sb_" + name, [128, cols], dt))

        xs = sb("xs", DC * T, F32)
        h = sb("h", DC * T, BF16)
        big = sb("big", FC * T, BF16)
        y = sb("y", DC * T, F32)
        gt = sb("gt", 4 * T, F32)
        wbuf = [sb("w%d" % s, WCOLS, BF16) for s in range(NSLOT)]
        rstd = sb("rstd", T, F32)
        rden = sb("rden", T, F32)
        rin = sb("rin", T, F32)
        meanb = rin
        onesf = sb("onesf", 1, F32)
        rcol = sb("rcol", 4, F32)
        tmp = [sb("tmp%d" % i, T, F32) for i in range(3)]
        sq = [sb("sq%d" % i, T, BF16) for i in range(4)]
        cbuf = [sb("cbuf%d" % i, 544, F32) for i in range(2)]
        pTT = sb("pTT", 2176, BF16)
        identb = sb("identb", 128, BF16)
        cbR = sb("cbR", DC, F32)
        vecs = sb("vecs", NV * DC, F32)
        ones = sb("ones", 128, BF16)
        wsT = sb("wsTb", 1024, BF16)
        Rb = sb("Rb", 1024, F32)
        sbias = sb("sbias", 1024, F32)
        haloA = sb("haloA", 2 * DC * 2, F32)
        haloC = sb("haloC", DC * 30, BF16)
        bst = sb("bst", 4 * 4 * 6, F32)
        mv = sb("mv", 8, F32)
        rs = sb("rs", 4, F32)
        ps = [st.enter_context(nc.psum_tensor("ps%d" % i, [128, 512], F32)) for i in range(8)]
        bank_ctr = [0]

        held = set()

        def bank():
            while True:
                b = bank_ctr[0] % 8
                bank_ctr[0] += 1
                if b not in held:
                    return b

        def vcol(vid, c):
            return vecs[:, vid * DC + c: vid * DC + c + 1]

        wq_list = []
        wstate = {'load': 0, 'use': 0}

        def w_prefetch():
            while wstate['load'] < len(wq_list) and wstate['load'] < wstate['use'] + NSLOT:
                i = wstate['load']
                ap, ncols = wq_list[i]
                s = i % NSLOT
                cx.dma('pool', 'w%d' % s,
                       lambda e, ap=ap, s=s, ncols=ncols: e.dma_start(out=wbuf[s][:, :ncols], in_=ap),
                       writes=[('w', s)])
                wstate['load'] += 1

        def w_get(ap):
            i = wstate['use']
            assert wq_list[i][0] is ap, "weight stream order mismatch at %d" % i
            w_prefetch()
            return i % NSLOT

        def w_done():
            wstate['use'] += 1
            w_prefetch()

        def plan_weights():
            def mixer_seq(i):
                kind, slot = i % 3, i // 3
                out = []
                if kind == 0:
                    for g in range(4):
                        out += [w_a_in[slot, g], w_a_in[slot, 4 + g], w_a_in[slot, 8 + g]]
                    out += [w_a_out[slot, n] for n in range(4)]
                elif kind == 1:
                    out += [w_b_in[slot, n] for n in range(8)]
                    out += [w_b_out[slot, n] for n in range(4)]
                else:
                    for g in range(4):
                        out += [w_c_in[slot, 4 + g], w_c_in[slot, g]]
                    out += [w_c_out[slot, n] for n in range(4)]
                return out
            seq = []
            for i in layers:
                for n in range(8):
                    seq += [(a, 4096) for a in w_kv[i, n]]
            per_tile = []
            for i in layers:
                for tl in mixer_seq(i):
                    per_tile += [(a, 4096) for a in tl]
                for n in range(4):
                    per_tile += [(a, 4096) for a in w_q[i, n]]
                for n in range(4):
                    per_tile += [(a, 4096) for a in w_o[i, n]]
                for n in range(22):
                    per_tile += [(a, 4096) for a in w_gu[i, n]]
                for n in range(8):
                    per_tile += [(a, 2816) for a in w_dn[i, n]]
            return seq, per_tile

        ap_cache = {}
        for (nm, t, shape) in [('a_in', w_a_in, (2, 12)), ('a_out', w_a_out, (2, 4)), ('b_in', w_b_in, (1, 8)),
                               ('b_out', w_b_out, (1, 4)), ('c_in', w_c_in, (1, 8)), ('c_out', w_c_out, (1, 4)),
                               ('q', w_q, (4, 4)), ('kv', w_kv, (4, 8)), ('o', w_o, (4, 4)), ('gu', w_gu, (4, 22))]:
            for a in range(shape[0]):
                for b in range(shape[1]):
                    ap_cache[(nm, a, b)] = [t[a, b, 0], t[a, b, 1]]
        for a in range(4):
            for b in range(8):
                ap_cache[('dn', a, b)] = [w_dn[a, b, c] for c in range(4)]

        class _W:
            def __init__(self, nm):
                self.nm = nm

            def __getitem__(self, idx):
                return ap_cache[(self.nm,) + tuple(idx)]
        w_a_in, w_a_out, w_b_in, w_b_out = _W('a_in'), _W('a_out'), _W('b_in'), _W('b_out')
        w_c_in, w_c_out, w_q, w_kv, w_o, w_gu, w_dn = _W('c_in'), _W('c_out'), _W('q'), _W('kv'), _W('o'), _W('gu'), _W('dn')
        pro_seq, tile_seq = plan_weights()
        wq_list.extend(pro_seq)
        for _ in range(NT):
            wq_list.extend(tile_seq)

        pending_pe = []
        marks = []

        def mark(lbl):
            marks.append((cx.ecnt['pe'], lbl))

        def flush_pe():
            items = list(pending_pe)
            pending_pe.clear()
            for f in items:
                f()

        def stat_mm(b, rhs_ap, rhs_res, first, last, ntok=T):
            cx.emit('pe', lambda e: e.matmul(ps[b][:, :ntok], ones[:, :], rhs_ap, start=first, stop=last),
                    reads=[rhs_res, 'consts'], writes=[('ps', b)])

        def recip(buf, bufn, ntok=T, scr=2):
            cx.emit('dve', lambda e: e.reciprocal(out=buf[:, :ntok], in_=buf[:, :ntok]), reads=[bufn], writes=[bufn])

        def finish_rstd(b, ntok=T, dst=None, dstn='rstd'):
            dst = rstd if dst is None else dst
            cx.emit('act', lambda e: e.activation(out=dst[:, :ntok], in_=ps[b][:, :ntok], func=AF.Ln, bias=EPS),
                    reads=[('ps', b)], writes=[dstn])
            cx.emit('act', lambda e: e.activation(out=dst[:, :ntok], in_=dst[:, :ntok], func=AF.Exp, scale=-0.5),
                    reads=[dstn], writes=[dstn])

        def rms_stats(src, srcn, ntok=T):
            b = bank()
            for c in range(DC):
                q = sq[c % 4]
                cx.emit('act', lambda e, c=c, q=q: e.activation(out=q[:, :ntok], in_=src[:, c * ntok:(c + 1) * ntok],
                                                                func=AF.Square, scale=RSD),
                        reads=[(srcn, c)], writes=[('sq', c % 4)])
                stat_mm(b, q[:, :ntok], ('sq', c % 4), c == 0, c == DC - 1, ntok)
            finish_rstd(b, ntok)

        def rms_apply(src, srcn, vid, ntok=T):
            for c in range(DC):
                cx.emit('dve', lambda e, c=c: e.scalar_tensor_tensor(
                    out=h[:, c * ntok:(c + 1) * ntok], in0=src[:, c * ntok:(c + 1) * ntok], scalar=vcol(vid, c),
                    in1=rstd[:, :ntok], op0=ALU.mult, op1=ALU.mult),
                    reads=[(srcn, c), 'rstd', 'consts'], writes=[('h', c)])

        def rms_pre2(vid, need_r2=False, need_col=False):
            for c in range(DC):
                cx.emit('act', lambda e, c=c: e.activation(out=h[:, c * T:(c + 1) * T], in_=xs[:, c * T:(c + 1) * T],
                                                           func=AF.Identity, scale=vcol(vid, c)),
                        reads=[('xs', c), 'consts'], writes=[('h', c)])
            b = bank()
            held.add(b)
            for c in range(DC):
                cx.emit('act', lambda e, c=c: e.activation(out=big[:, c * T:(c + 1) * T], in_=xs[:, c * T:(c + 1) * T],
                                                           func=AF.Square, scale=RSD),
                        reads=[('xs', c)], writes=[('big', c)])
            for i in range(DC // 2):
                c0, c1 = 2 * i, 2 * i + 1
                cx.emit('dve', lambda e, c0=c0, c1=c1: e.tensor_tensor(
                    out=big[:, c0 * T:(c0 + 1) * T], in0=big[:, c0 * T:(c0 + 1) * T], in1=big[:, c1 * T:(c1 + 1) * T], op=ALU.add),
                    reads=[('big', c0), ('big', c1)], writes=[('big', c0)])
                pending_pe.append(lambda c0=c0, i=i: stat_mm(b, big[:, c0 * T:(c0 + 1) * T], ('big', c0), i == 0, i == DC // 2 - 1))

            def fin():
                finish_rstd(b, T, rin, 'rin')
                held.discard(b)
                if need_r2:
                    cx.emit('dve', lambda e: e.tensor_tensor(out=rden[:, :], in0=rin[:, :], in1=rin[:, :], op=ALU.mult),
                            reads=['rin'], writes=['rden'])
                if need_col:
                    bb = bank()
                    for tc in range(4):
                        cx.emit('pe', lambda e, tc=tc: e.matmul(ps[bb][:, tc:tc + 1], rin[:, tc * 128:(tc + 1) * 128], onesf[:, 0:1],
                                                                start=True, stop=True),
                                reads=['rin', 'consts'], writes=[('ps', bb)])
                    cx.emit('act', lambda e: e.activation(out=rcol[:, 0:4], in_=ps[bb][:, 0:4], func=AF.Copy),
                            reads=[('ps', bb)], writes=['rcol'])
            pending_pe.append(fin)

        def mul_rin(out_ap, out_res, b, src=None, srcn='rin'):
            src = rin if src is None else src
            cx.emit('dve', lambda e: e.tensor_tensor(out=out_ap, in0=ps[b][:, :], in1=src[:, :], op=ALU.mult),
                    reads=[('ps', b), srcn], writes=[out_res])

        def proj_fm(src, srcn, ntok, wtiles, kc, evac, nj=4):
            ncol = nj * 128
            for ng, kts in enumerate(wtiles):
                banks = [bank() for _ in range(nj)]
                held.update(banks)
                nkt = len(kts)
                for kt, ap in enumerate(kts):
                    s = w_get(ap)
                    for j in range(nj):
                        for k in range(kc):
                            kk = kt * kc + k
                            cx.emit('pe', lambda e, s=s, j=j, k=k, kk=kk, b=banks[j]: e.matmul(
                                ps[b][:, :ntok], wbuf[s][:, k * ncol + j * 128: k * ncol + (j + 1) * 128],
                                src[:, kk * ntok:(kk + 1) * ntok],
                                start=(kt == 0 and k == 0), stop=(kt == nkt - 1 and k == kc - 1)),
                                reads=[('w', s), (srcn, kk)], writes=[('ps', banks[j])])
                    w_done()
                flush_pe()
                for j in range(nj):
                    evac(ng * nj + j, banks[j])
                for f in late_evacs:
                    f()
                late_evacs.clear()
                for j in range(nj):
                    held.discard(banks[j])
            flush_pe()

        stat_bank = [0]

        def evac_y_begin():
            stat_bank[0] = bank()
            held.add(stat_bank[0])

        post_vid = [0]

        late_evacs = []

        def evac_y(c, b):
            q = sq[c % 4]
            cx.emit('act', lambda e: e.activation(out=q[:, :], in_=ps[b][:, :], func=AF.Square, scale=RSD),
                    reads=[('ps', b)], writes=[('sq', c % 4)])
            vid = post_vid[0]
            late_evacs.append(lambda: cx.emit('act', lambda e: e.activation(
                out=y[:, c * T:(c + 1) * T], in_=ps[b][:, :], func=AF.Identity, scale=vcol(vid, c)),
                reads=[('ps', b), 'consts'], writes=[('y', c)]))
            sbk = stat_bank[0]
            if c % 2 == 1:
                q0 = sq[(c - 1) % 4]
                cx.emit('dve', lambda e: e.tensor_tensor(out=q0[:, :], in0=q0[:, :], in1=q[:, :], op=ALU.add),
                        reads=[('sq', (c - 1) % 4), ('sq', c % 4)], writes=[('sq', (c - 1) % 4)])
                pending_pe.append(lambda: stat_mm(sbk, q0[:, :], ('sq', (c - 1) % 4), c == 1, c == DC - 1))

        def post_norm(vid):
            flush_pe()
            finish_rstd(stat_bank[0])
            held.discard(stat_bank[0])
            assert vid == post_vid[0]
            if NWARM:
                bw = bank()
                for _ in range(NWARM):
                    cx.emit('pe', lambda e: e.matmul(ps[bw][:, :], ones[:, :], wsT[:, 0:512], start=True, stop=True),
                            reads=['consts'], writes=[('ps', bw)])
            for c in range(DC):
                en = 'dve'
                cx.emit(en, lambda e, c=c: e.tensor_tensor(
                    out=y[:, c * T:(c + 1) * T], in0=y[:, c * T:(c + 1) * T], in1=rstd[:, :], op=ALU.mult),
                    reads=[('y', c), 'rstd'], writes=[('y', c)])
                cx.emit(en, lambda e, c=c: e.tensor_tensor(
                    out=xs[:, c * T:(c + 1) * T], in0=xs[:, c * T:(c + 1) * T], in1=y[:, c * T:(c + 1) * T], op=ALU.add),
                    reads=[('xs', c), ('y', c)], writes=[('xs', c)])

        def out_proj(wt, nm, i_slot, vid):
            mark('outproj_' + nm)
            post_vid[0] = vid
            evac_y_begin()
            proj_fm(big, 'big', T, [wt[i_slot, n] for n in range(4)], 8, evac_y)
            post_norm(vid)

        def conv_chunk(cb, cbn, K, vid_w, chunk, halo_ap, halo_res, first_bias_vid=None, out_ap=None, out_res=None):
            H = K - 1
            cx.emit('dve', lambda e: e.tensor_copy(out=cb[:, 0:H], in_=halo_ap), reads=[halo_res], writes=[cbn])
            cx.emit('dve', lambda e: e.tensor_copy(out=halo_ap, in_=cb[:, T:T + H]), reads=[cbn], writes=[halo_res])
            if first_bias_vid is None:
                cx.emit('dve', lambda e: e.tensor_scalar(out=out_ap, in0=cb[:, H:H + T], scalar1=vcol(vid_w(K - 1), chunk),
                                                         scalar2=None, op0=ALU.mult),
                        reads=[cbn, 'consts'], writes=[out_res])
            else:
                cx.emit('dve', lambda e: e.tensor_scalar(out=out_ap, in0=cb[:, H:H + T], scalar1=vcol(vid_w(K - 1), chunk),
                                                         scalar2=vcol(first_bias_vid, chunk), op0=ALU.mult, op1=ALU.add),
                        reads=[cbn, 'consts'], writes=[out_res])
            for k in range(K - 2, -1, -1):
                cx.emit('dve', lambda e, k=k: e.scalar_tensor_tensor(
                    out=out_ap, in0=cb[:, k:k + T], scalar=vcol(vid_w(k), chunk), in1=out_ap,
                    op0=ALU.mult, op1=ALU.add),
                    reads=[cbn, out_res, 'consts'], writes=[out_res])

        def mixer_a(i):
            mark('A_in')
            slot = i // 3
            rms_pre2(V_MIX(i, 0), need_r2=True)
            for g in range(4):
                def ev_b(c, b):
                    j = c % 4
                    mul_rin(gt[:, j * T:(j + 1) * T], ('gt', j), b)

                def ev_c(c, b):
                    j = c % 4
                    mul_rin(y[:, j * T:(j + 1) * T], ('y', j), b, rden, 'rden')

                def ev_z(c, b, g=g):
                    j = c % 4
                    chunk = g * 4 + j
                    cb = cbuf[chunk % 2]
                    cbn = ('cbuf', chunk % 2)
                    cx.emit('dve', lambda e: e.tensor_tensor(out=cb[:, 2:2 + T], in0=y[:, j * T:(j + 1) * T],
                                                             in1=ps[b][:, :], op=ALU.mult),
                            reads=[('y', j), ('ps', b)], writes=[cbn])
                    hoff = (slot * DC + chunk) * 2
                    t_ = tmp[chunk % 3]
                    conv_chunk(cb, cbn, 3, lambda k: V_ACONV(slot, k), chunk, haloA[:, hoff:hoff + 2], ('haloA', slot, chunk),
                               out_ap=t_[:, :], out_res=('tmp', chunk % 3))
                    cx.emit('dve', lambda e: e.tensor_tensor(out=big[:, chunk * T:(chunk + 1) * T], in0=t_[:, :],
                                                             in1=gt[:, j * T:(j + 1) * T], op=ALU.mult),
                            reads=[('tmp', chunk % 3), ('gt', j)], writes=[('big', chunk)])
                proj_fm(h, 'h', T, [w_a_in[slot, g]], 8, ev_b)
                proj_fm(h, 'h', T, [w_a_in[slot, 4 + g]], 8, ev_c)
                proj_fm(h, 'h', T, [w_a_in[slot, 8 + g]], 8, ev_z)
            out_proj(w_a_out, 'a_out', slot, V_MIX(i, 1))

        def mixer_c(i):
            mark('C_in')
            slot = i // 3
            rms_pre2(V_MIX(i, 0))
            b1, b2 = bank(), bank()
            held.add(b1)
            held.add(b2)
            DG0 = 16

            def dg_load(c):
                s3 = c % 3
                base = (DG0 + 8 * s3) * T
                cx.dma('sp', 'dg%d' % s3, lambda e: e.dma_start(out=big[:, base:base + 3968], in_=dgc[c]),
                       reads=[('dgc', c)], writes=[('big', DG0 + 8 * s3 + n) for n in range(8)])
            for c in range(3):
                dg_load(c)
            prev_stat = [None]

            def run_prev_stat():
                if prev_stat[0] is not None:
                    prev_stat[0]()
                    prev_stat[0] = None

            for g in range(4):
                def ev_g(c, b):
                    j = c % 4
                    mul_rin(gt[:, j * T:(j + 1) * T], ('gt', j), b)
                    cx.emit('act', lambda e: e.activation(out=gt[:, j * T:(j + 1) * T], in_=gt[:, j * T:(j + 1) * T], func=AF.Sigmoid),
                            reads=[('gt', j)], writes=[('gt', j)])

                def ev_a(c, b, g=g):
                    j = c % 4
                    chunk = g * 4 + j
                    off = j * 544
                    cbn = ('cbb', j)
                    t_ = tmp[j % 3]
                    mul_rin(t_[:, :], ('tmp', j % 3), b)
                    cx.emit('dve', lambda e: e.tensor_tensor(out=pTT[:, off + 30: off + 30 + T], in0=gt[:, j * T:(j + 1) * T],
                                                             in1=t_[:, :], op=ALU.mult),
                            reads=[('gt', j), ('tmp', j % 3)], writes=[cbn])
                    hres = ('haloC', chunk)
                    cx.emit('dve', lambda e: e.tensor_copy(out=pTT[:, off: off + 30], in_=haloC[:, chunk * 30:(chunk + 1) * 30]),
                            reads=[hres], writes=[cbn])
                    cx.emit('dve', lambda e: e.tensor_copy(out=haloC[:, chunk * 30:(chunk + 1) * 30], in_=pTT[:, off + T: off + T + 30]),
                            reads=[cbn], writes=[hres])

                    def conv():
                        s3 = chunk % 3
                        base = (DG0 + 8 * s3) * T
                        dgres = [('big', DG0 + 8 * s3 + n) for n in range(8)]
                        bc = bank()
                        for k in range(31):
                            cx.emit('pe', lambda e, k=k: e.matmul(ps[bc][:, :], big[:, base + k * 128: base + (k + 1) * 128],
                                                                  pTT[:, off + k: off + k + T], start=(k == 0), stop=(k == 30)),
                                    reads=dgres + [cbn], writes=[('ps', bc)])
                        if chunk + 3 < DC:
                            dg_load(chunk + 3)
                        run_prev_stat()
                        q1, q2 = sq[(2 * chunk) % 4], sq[(2 * chunk + 1) % 4]
                        cx.emit('act', lambda e: e.activation(out=y[:, chunk * T:(chunk + 1) * T], in_=ps[bc][:, :], func=AF.Identity,
                                                              bias=vcol(V_CB, chunk)),
                                reads=[('ps', bc), 'consts'], writes=[('y', chunk)])
                        cx.emit('act', lambda e: e.activation(out=q1[:, :], in_=ps[bc][:, :], func=AF.Identity,
                                                              bias=vcol(V_CB, chunk)),
                                reads=[('ps', bc), 'consts'], writes=[('sq', (2 * chunk) % 4)])
                        cx.emit('act', lambda e: e.activation(out=q2[:, :], in_=ps[bc][:, :], func=AF.Square, scale=RSD,
                                                              bias=cbR[:, chunk:chunk + 1]),
                                reads=[('ps', bc), 'consts'], writes=[('sq', (2 * chunk + 1) % 4)])

                        def stat():
                            stat_mm(b1, q1[:, :], ('sq', (2 * chunk) % 4), chunk == 0, chunk == DC - 1)
                            stat_mm(b2, q2[:, :], ('sq', (2 * chunk + 1) % 4), chunk == 0, chunk == DC - 1)
                        prev_stat[0] = stat
                    pending_pe.append(conv)
                proj_fm(h, 'h', T, [w_c_in[slot, 4 + g]], 8, ev_g)
                proj_fm(h, 'h', T, [w_c_in[slot, g]], 8, ev_a)
            flush_pe()
            run_prev_stat()
            cx.emit('act', lambda e: e.activation(out=meanb[:, :], in_=ps[b1][:, :], func=AF.Copy, scale=1.0 / D),
                    reads=[('ps', b1)], writes=['rin'])
            cx.emit('dve', lambda e: e.tensor_tensor(out=rden[:, :], in0=meanb[:, :], in1=meanb[:, :], op=ALU.mult),
                    reads=['rin'], writes=['rden'])
            cx.emit('dve', lambda e: e.tensor_tensor(out=rstd[:, :], in0=ps[b2][:, :], in1=rden[:, :], op=ALU.subtract),
                    reads=[('ps', b2), 'rden'], writes=['rstd'])
            held.discard(b1)
            held.discard(b2)
            cx.emit('act', lambda e: e.activation(out=rstd[:, :], in_=rstd[:, :], func=AF.Sqrt, bias=EPS),
                    reads=['rstd'], writes=['rstd'])
            cx.emit('dve', lambda e: e.reciprocal(out=rstd[:, :], in_=rstd[:, :]), reads=['rstd'], writes=['rstd'])
            for c in range(DC):
                t_ = tmp[c % 3]
                cx.emit('dve', lambda e, c=c, t_=t_: e.tensor_tensor(out=t_[:, :], in0=y[:, c * T:(c + 1) * T],
                                                                     in1=meanb[:, :], op=ALU.subtract),
                        reads=[('y', c), 'rin'], writes=[('tmp', c % 3)])
                cx.emit('dve', lambda e, t_=t_: e.tensor_tensor(out=t_[:, :], in0=t_[:, :], in1=rstd[:, :], op=ALU.mult),
                        reads=[('tmp', c % 3), 'rstd'], writes=[('tmp', c % 3)])
                cx.emit('act', lambda e, c=c, t_=t_: e.activation(out=big[:, c * T:(c + 1) * T], in_=t_[:, :], func=AF.Silu,
                                                                  scale=vcol(V_CLG, c), bias=vcol(V_CLB, c)),
                        reads=[('tmp', c % 3), 'consts'], writes=[('big', c)])
            out_proj(w_c_out, 'c_out', slot, V_MIX(i, 1))

        VH = DC * T

        def mixer_b(i):
            mark('B_in')
            slot = i // 3
            rms_pre2(V_MIX(i, 0), need_col=True)

            def ev_u(c, b):
                t_ = tmp[c % 2]
                mul_rin(t_[:, :], ('tmp', c % 2), b)
                cx.emit('act', lambda e: e.activation(out=big[:, c * T:(c + 1) * T], in_=t_[:, :], func=AF.Gelu_apprx_tanh),
                        reads=[('tmp', c % 2)], writes=[('big', c)])
            proj_fm(h, 'h', T, [w_b_in[slot, n] for n in range(4)], 8, ev_u)
            for n in range(4):
                banks = [bank() for _ in range(4)]
                for kt, ap in enumerate(w_b_in[slot, 4 + n]):
                    s = w_get(ap)
                    for tc in range(4):
                        for k in range(8):
                            kk = kt * 8 + k
                            cx.emit('pe', lambda e, s=s, tc=tc, k=k, kk=kk, b=banks[tc]: e.matmul(
                                ps[b][:, :], h[:, kk * T + tc * 128: kk * T + (tc + 1) * 128], wbuf[s][:, k * 512:(k + 1) * 512],
                                start=(kk == 0), stop=(kk == DC - 1)),
                                reads=[('w', s), ('h', kk)], writes=[('ps', banks[tc])])
                    w_done()
                for tc in range(4):
                    cx.emit('act', lambda e, tc=tc, n=n, b=banks[tc]: e.activation(
                        out=y[:, tc * 2048 + n * 512: tc * 2048 + (n + 1) * 512], in_=ps[b][:, :], func=AF.Gelu_apprx_tanh,
                        scale=rcol[:, tc:tc + 1]),
                        reads=[('ps', banks[tc]), 'rcol'], writes=[('y', tc * 4 + n)])
            for tc in range(4):
                for n in range(4):
                    cx.emit('dve', lambda e, tc=tc, n=n: e.bn_stats(out=bst[:, (tc * 4 + n) * 6:(tc * 4 + n + 1) * 6],
                                                                    in_=y[:, tc * 2048 + n * 512: tc * 2048 + (n + 1) * 512]),
                            reads=[('y', tc * 4 + n)], writes=[('bst', tc * 4 + n)])
                cx.emit('dve', lambda e, tc=tc: e.bn_aggr(out=mv[:, tc * 2:tc * 2 + 2], in_=bst[:, tc * 24:(tc + 1) * 24]),
                        reads=[('bst', tc * 4 + n) for n in range(4)], writes=[('mv', tc)])
                cx.emit('act', lambda e, tc=tc: e.activation(out=rs[:, tc:tc + 1], in_=mv[:, tc * 2 + 1:tc * 2 + 2], func=AF.Sqrt, bias=EPS),
                        reads=[('mv', tc)], writes=[('rs', tc)])
                cx.emit('dve', lambda e, tc=tc: e.reciprocal(out=rs[:, tc:tc + 1], in_=rs[:, tc:tc + 1]),
                        reads=[('rs', tc)], writes=[('rs', tc)])
                cx.emit('dve', lambda e, tc=tc: e.tensor_scalar(
                    out=big[:, VH + tc * 2048: VH + (tc + 1) * 2048], in0=y[:, tc * 2048:(tc + 1) * 2048],
                    scalar1=mv[:, tc * 2:tc * 2 + 1], scalar2=rs[:, tc:tc + 1], op0=ALU.subtract, op1=ALU.mult),
                    reads=[('y', tc * 4 + n) for n in range(4)] + [('mv', tc), ('rs', tc)],
                    writes=[('big', DC + tc * 4 + n) for n in range(4)])
            for c in range(DC):
                g = c // 2
                b = bank()
                for n in range(4):
                    cx.emit('pe', lambda e, n=n, c=c, g=g, b=b: e.matmul(
                        ps[b][:, n * 128:(n + 1) * 128], big[:, VH + n * 2048 + c * 128: VH + n * 2048 + (c + 1) * 128],
                        wsT[:, g * 128:(g + 1) * 128], start=True, stop=True),
                        reads=[('big', DC + n * 4 + c // 4), 'consts'], writes=[('ps', b)])
                t0, t1 = tmp[0], tmp[1]
                cx.emit('dve', lambda e, c=c, g=g: e.scalar_tensor_tensor(
                    out=t0[:, 0:128], in0=Rb[:, g * 128:(g + 1) * 128], scalar=vcol(V_BVB, c),
                    in1=sbias[:, g * 128:(g + 1) * 128], op0=ALU.mult, op1=ALU.add),
                    reads=['consts'], writes=[('tmp', 0)])
                for n in range(4):
                    cx.emit('dve', lambda e, c=c, n=n, b=b: e.scalar_tensor_tensor(
                        out=t1[:, n * 128:(n + 1) * 128], in0=ps[b][:, n * 128:(n + 1) * 128], scalar=vcol(V_BVG, c),
                        in1=t0[:, 0:128], op0=ALU.mult, op1=ALU.add),
                        reads=[('ps', b), ('tmp', 0), 'consts'], writes=[('tmp', 1)])
                cx.emit('dve', lambda e, c=c: e.tensor_tensor(out=big[:, c * T:(c + 1) * T], in0=big[:, c * T:(c + 1) * T],
                                                              in1=t1[:, :], op=ALU.mult),
                        reads=[('big', c), ('tmp', 1)], writes=[('big', c)])
            out_proj(w_b_out, 'b_out', slot, V_MIX(i, 1))

        KT0 = DC * T
        V0 = DC * T + 4096
        SCALE = 512.0 ** -0.5

        def xattn(i):
            mark('xattn_q')
            cx.dma('sp', 'kvld', lambda e: e.dma_start(out=big[:, KT0:KT0 + 8192], in_=kvc[i]),
                   reads=[('kvc', i)], writes=[('big', DC + n) for n in range(16)])
            rms_pre2(V_XA(i, 0))

            def ev_q(c, b):
                mul_rin(big[:, c * T:(c + 1) * T], ('big', c), b)
            proj_fm(h, 'h', T, [w_q[i, n] for n in range(4)], 8, ev_q)
            kvres = [('big', DC + n) for n in range(16)]
            mark('xattn_core')
            for hd in range(4):
                po = (hd % 2) * 1024
                pn = ('pT', hd % 2)
                for mc in range(2):
                    b = bank()
                    for j in range(4):
                        ch = hd * 4 + j
                        cx.emit('pe', lambda e, ch=ch, mc=mc, j=j, b=b: e.matmul(
                            ps[b][:, :], big[:, KT0 + ch * 256 + mc * 128: KT0 + ch * 256 + (mc + 1) * 128],
                            big[:, ch * T:(ch + 1) * T], start=(j == 0), stop=(j == 3)),
                            reads=kvres + [('big', ch)], writes=[('ps', b)])
                    cx.emit('act', lambda e, mc=mc, b=b, po=po: e.activation(out=pTT[:, po + mc * T: po + (mc + 1) * T], in_=ps[b][:, :],
                                                                           func=AF.Exp, scale=SCALE),
                            reads=[('ps', b)], writes=[(pn, mc)])
                bd = bank()
                for mc in range(2):
                    cx.emit('pe', lambda e, mc=mc, po=po: e.matmul(ps[bd][:, :], ones[:, :], pTT[:, po + mc * T: po + (mc + 1) * T],
                                                                   start=(mc == 0), stop=(mc == 1)),
                            reads=[(pn, mc), 'consts'], writes=[('ps', bd)])
                cx.emit('dve', lambda e: e.reciprocal(out=rden[:, :], in_=ps[bd][:, :]), reads=[('ps', bd)], writes=['rden'])
                for j in range(4):
                    ch = hd * 4 + j
                    b = bank()
                    for mc in range(2):
                        cx.emit('pe', lambda e, ch=ch, mc=mc, b=b, po=po: e.matmul(
                            ps[b][:, :], big[:, V0 + mc * 2048 + ch * 128: V0 + mc * 2048 + (ch + 1) * 128],
                            pTT[:, po + mc * T: po + (mc + 1) * T], start=(mc == 0), stop=(mc == 1)),
                            reads=kvres + [(pn, mc)], writes=[('ps', b)])
                    cx.emit('dve', lambda e, ch=ch, b=b: e.tensor_tensor(out=big[:, ch * T:(ch + 1) * T], in0=ps[b][:, :],
                                                                        in1=rden[:, :], op=ALU.mult),
                            reads=[('ps', b), 'rden'], writes=[('big', ch)])
            out_proj(w_o, 'o', i, V_XA(i, 1))

        def ffn(i):
            mark('ffn_gu')
            rms_pre2(V_FFN(i, 0))
            for grp in range(22):
                def ev_gu(c, b, grp=grp):
                    j = c % 4
                    if j < 2:
                        t_ = tmp[j]
                        mul_rin(t_[:, :], ('tmp', j), b)
                        cx.emit('act', lambda e: e.activation(out=t_[:, :], in_=t_[:, :], func=AF.Silu),
                                reads=[('tmp', j)], writes=[('tmp', j)])
                    else:
                        t_ = tmp[j - 2]
                        hc = grp * 2 + (j - 2)
                        mul_rin(gt[:, (j - 2) * T:(j - 1) * T], ('gt', j - 2), b)
                        cx.emit('dve', lambda e: e.tensor_tensor(out=big[:, hc * T:(hc + 1) * T], in0=t_[:, :],
                                                                 in1=gt[:, (j - 2) * T:(j - 1) * T], op=ALU.mult),
                                reads=[('tmp', j - 2), ('gt', j - 2)], writes=[('big', hc)])
                proj_fm(h, 'h', T, [w_gu[i, grp]], 8, ev_gu)
            mark('ffn_down')
            post_vid[0] = V_FFN(i, 1)
            evac_y_begin()
            proj_fm(big, 'big', T, [w_dn[i, n] for n in range(8)], 11, evac_y, nj=2)
            post_norm(V_FFN(i, 1))

        def load_x(t):
            for c in range(DC):
                cx.dma('sp', 'xld%d' % c, lambda e, c=c: e.dma_start(out=xs[:, c * T:(c + 1) * T], in_=xT[t, :, c * T:(c + 1) * T]),
                       writes=[('xs', c)])

        load_x(0)
        cx.dma('sp', 'cst', lambda e: e.dma_start(out=vecs[:, :], in_=vecs_d), writes=['consts'])
        cx.dma('sp', 'cst', lambda e: e.dma_start(out=sbias[:, :], in_=sbias_d), writes=['consts'])
        cx.dma('sp', 'cst', lambda e: e.dma_start(out=y[:, 0:1024], in_=wsT_d), writes=['consts', ('y', 0), ('y', 1)])
        cx.emit('dve', lambda e: e.memset(ones[:, :], 1.0), writes=['consts'])
        cx.emit('dve', lambda e: e.memset(onesf[:, :], 1.0 / 128.0), writes=['consts'])
        cx.emit('dve', lambda e: e.memset(haloA[:, :], 0.0), writes=[('haloA', s_, c) for s_ in range(2) for c in range(DC)])
        cx.emit('dve', lambda e: e.memset(haloC[:, :], 0.0), writes=[('haloC', c) for c in range(DC)])
        if 1 in layers:
            cx.emit('pool', lambda e: e.affine_select(out=wsT[:, :], in_=y[:, 0:1024], pattern=[[0, 8], [1, 128]],
                                                       compare_op=ALU.is_ge, fill=0.0, base=0, channel_multiplier=-1),
                    reads=['consts', ('y', 0), ('y', 1)], writes=['consts'])
            for hf in range(2):
                b = bank()
                for g4 in range(4):
                    g = hf * 4 + g4
                    cx.emit('pe', lambda e, g=g, g4=g4, b=b: e.matmul(ps[b][:, g4 * 128:(g4 + 1) * 128], ones[:, :],
                                                                      wsT[:, g * 128:(g + 1) * 128], start=True, stop=True),
                            reads=['consts'], writes=[('ps', b)])
                cx.emit('act', lambda e, hf=hf, b=b: e.activation(out=Rb[:, hf * 512:(hf + 1) * 512], in_=ps[b][:, :], func=AF.Copy),
                        reads=[('ps', b)], writes=['consts'])
        def dg_setup():
            cx.emit('dve', lambda e: e.memset(identb[:, :], 1.0), writes=['identb'])
            cx.emit('pool', lambda e: e.affine_select(out=identb[:, :], in_=identb[:, :], pattern=[[1, 128]],
                                                       compare_op=ALU.is_equal, fill=0.0, base=0, channel_multiplier=-1),
                    reads=['identb'], writes=['identb'])
            cx.emit('dve', lambda e: e.tensor_scalar(out=cbR[:, 0:DC], in0=vecs[:, V_CB * DC:(V_CB + 1) * DC], scalar1=RSD,
                                                     scalar2=None, op0=ALU.mult),
                    reads=['consts'], writes=['consts'])

        def dg_build(c):
            sl = c % 2
            base = (28 + sl * 8) * T
            for k in range(31):
                cx.emit('dve', lambda e, k=k: e.tensor_scalar(
                    out=big[:, base + k * 128: base + (k + 1) * 128], in0=identb[:, :], scalar1=vcol(V_CCONV(k), c),
                    scalar2=None, op0=ALU.mult),
                    reads=['identb', 'consts'], writes=[('big', 28 + sl * 8 + k // 4)])
            cx.dma('sp', 'dgst', lambda e: e.dma_start(out=dgc[c], in_=big[:, base:base + 3968]),
                   reads=[('big', 28 + sl * 8 + n) for n in range(8)], writes=[('dgc', c)])

        dg_todo = list(range(DC)) if 2 in layers else []
        if dg_todo:
            dg_setup()
        cx.dma('sp', 'mld', lambda e: e.dma_start(out=y[:, 0:DC * NMEM], in_=memT),
               writes=[('y', c) for c in range(DC)])
        rms_stats(y, 'y', ntok=NMEM)
        for i in layers:
            rms_apply(y, 'y', V_XA(i, 2), ntok=NMEM)
            nshare = (len(dg_todo) + (len(layers) - layers.index(i)) - 1) // (len(layers) - layers.index(i))
            for _ in range(nshare):
                dg_build(dg_todo.pop(0))

            def ev_k(c, b):
                cx.emit('act', lambda e: e.activation(out=big[:, c * 256:(c + 1) * 256], in_=ps[b][:, :256], func=AF.Copy),
                        reads=[('ps', b)], writes=[('big', c)])
            proj_fm(h, 'h', NMEM, [w_kv[i, n] for n in range(4)], 8, ev_k)
            for n in range(4):
                banks = [bank() for _ in range(2)]
                for kt, ap in enumerate(w_kv[i, 4 + n]):
                    s = w_get(ap)
                    for tc in range(2):
                        for k in range(8):
                            kk = kt * 8 + k
                            cx.emit('pe', lambda e, s=s, tc=tc, k=k, kk=kk, b=banks[tc]: e.matmul(
                                ps[b][:, :], h[:, kk * NMEM + tc * 128: kk * NMEM + (tc + 1) * 128], wbuf[s][:, k * 512:(k + 1) * 512],
                                start=(kk == 0), stop=(kk == DC - 1)),
                                reads=[('w', s), ('h', kk)], writes=[('ps', banks[tc])])
                    w_done()
                for tc in range(2):
                    cx.emit('act', lambda e, tc=tc, n=n, b=banks[tc]: e.activation(
                        out=big[:, 4096 + tc * 2048 + n * 512: 4096 + tc * 2048 + (n + 1) * 512], in_=ps[b][:, :], func=AF.Copy),
                        reads=[('ps', banks[tc])], writes=[('big', 16 + tc * 4 + n)])
            cx.dma('sp', 'kvst', lambda e, i=i: e.dma_start(out=kvc[i], in_=big[:, 0:8192]),
                   reads=[('big', c) for c in range(24)], writes=[('kvc', i)])

        assert not dg_todo
        if 2 in layers:
            for c in range(DC):
                cx.last_w[('dgc', c)] = cx.last_w[('dgc', DC - 1)]
        for t in range(NT):
            if t > 0:
                load_x(t)
            for i in layers:
                [mixer_a, mixer_b, mixer_c][i % 3](i)
                if stop == 'mixer':
                    break
                xattn(i)
                if stop == 'xattn':
                    break
                ffn(i)
            for c in range(DC):
                cx.dma('sp', 'ost%d' % c, lambda e, t=t, c=c: e.dma_start(out=outT[t, :, c * T:(c + 1) * T], in_=xs[:, c * T:(c + 1) * T]),
                       reads=[('xs', c)], writes=[('out', t, c)])
        if dbg:
            allres = list(cx.last_w.keys())
            for (nm, buf, cols, dt) in [('h', h, DC * T, BF16), ('y', y, DC * T, F32), ('big', big, FC * T, BF16),
                                        ('rstd', rstd, T, F32), ('vecs', vecs, NV * DC, F32), ('gt', gt, 4 * T, F32)]:
                dd = nc.dram_tensor("dbg_" + nm, [128, cols], dt, kind="ExternalOutput").ap()
                cx.dma('sp', 'ost', lambda e, dd=dd, buf=buf: e.dma_start(out=dd, in_=buf[:, :]), reads=allres, writes=[('out', 'dbg' + nm)])
            cx.wait_all('sp', [('out', 'dbg' + nm) for nm in ['h', 'y', 'big', 'rstd', 'vecs', 'gt']])
        cx.wait_all('sp', [('out', t, c) for t in range(NT) for c in range(DC)])
        assert stop or wstate['use'] == len(wq_list), (wstate, len(wq_list))
        print("sbuf bytes remaining", nc.sbuf_bytes_remaining)
        print("instructions", cx.n_inst, "waits", cx.n_wait, "pe", cx.ecnt['pe'])
    return nc


def tile_w(w, kc, ncol=512):
    K, N = w.shape
    nkt = K // (128 * kc)
    ng = N // ncol
    a = w.reshape(nkt, kc, 128, ng, ncol).transpose(3, 0, 2, 1, 4)
    return np.ascontiguousarray(a).reshape(ng, nkt, 128, kc * ncol)


def prep_shared(inp):
    f = lambda a: np.asarray(a, dtype=np.float32)
    sh = {}
    rows = [f(inp['mix_norm']).reshape(8, D), f(inp['xa_norm']).reshape(12, D), f(inp['ffn_norm']).reshape(8, D),
            f(inp['a_conv_w']).reshape(6, D), f(inp['c_conv_w']).reshape(31, D), f(inp['c_conv_b']).reshape(1, D),
            f(inp['c_ln_g']).reshape(1, D), f(inp['c_ln_b']).reshape(1, D), f(inp['b_v_g']).reshape(1, D),
            f(inp['b_v_b']).reshape(1, D)]
    allv = np.concatenate(rows, axis=0)
    assert allv.shape[0] == NV
    sh['vecs'] = np.ascontiguousarray(allv.reshape(NV, DC, 128).transpose(2, 0, 1)).reshape(128, NV * DC)
    ws = f(inp['b_w_s'])[0]
    sh['wsT'] = np.ascontiguousarray(ws.transpose(2, 0, 1)).reshape(128, 8 * 128)
    sb_ = f(inp['b_s_bias'])[0].reshape(1, 8 * 128)
    sh['sbias'] = np.ascontiguousarray(np.broadcast_to(sb_, (128, 1024)))

    def tw(name, kc=8):
        w = f(inp[name])
        return np.stack([tile_w(w[l], kc) for l in range(w.shape[0])])
    for name in ['a_w_in', 'a_w_out', 'b_w_in', 'b_w_out', 'c_w_in', 'c_w_out', 'xa_wq', 'xa_wkv', 'xa_wo']:
        sh[name] = tw(name)
    gu = f(inp['ffn_w_gu'])
    gl = []
    for l in range(gu.shape[0]):
        gate = gu[l][:, :DFF].reshape(D, 22, 256)
        up = gu[l][:, DFF:].reshape(D, 22, 256)
        wi = np.concatenate([gate, up], axis=2).reshape(D, 2 * DFF)
        gl.append(tile_w(wi, 8))
    sh['ffn_w_gu'] = np.stack(gl)
    dn = f(inp['ffn_w_down'])
    sh['ffn_w_down'] = np.stack([tile_w(dn[l], 11, 256) for l in range(dn.shape[0])])
    return sh


def prep_core(x_b, mem_b, NT):
    xt = x_b[:NT * T].reshape(NT, T, DC, 128).transpose(0, 3, 2, 1)
    xt = np.ascontiguousarray(xt).reshape(NT, 128, DC * T)
    mt = np.ascontiguousarray(mem_b.reshape(NMEM, DC, 128).transpose(2, 1, 0)).reshape(128, DC * NMEM)
    return {'xT': xt, 'memT': mt}


def unprep_out(o, NT):
    return np.ascontiguousarray(o.reshape(NT, 128, DC, T).transpose(0, 3, 2, 1)).reshape(NT * T, D)


def kernel(**inputs):
    NT = SEQ // T
    x = np.asarray(inputs['x'], dtype=np.float32)
    mem = np.asarray(inputs['mem'], dtype=np.float32)
    sh = prep_shared(inputs)
    nc = build(NT, [0, 1, 2, 3])
    in_maps = []
    for b in range(8):
        m = dict(sh)
        m.update(prep_core(x[b], mem[b], NT))
        in_maps.append(m)
    res = run_bass_kernel_spmd(nc, in_maps, core_ids=list(range(8)))
    out = np.stack([unprep_out(np.asarray(res.results[b]['outT']), NT) for b in range(8)])
    return out.astype(np.float32)
```
